# Optimizing a Trainium2 kernel written in Bass

```python
import math
import jax, jax.numpy as jnp
from jax import lax
import numpy as np

D_MODEL = 2048
BATCH = 16
SEQ = 2048
DEPTH = 1

ATTN_HEADS = 16
ATTN_KV_HEADS = 4
ATTN_HEAD_DIM = 64
ATTN_GROUP = ATTN_HEADS // ATTN_KV_HEADS
WINDOW = 128
ATTN_BLOCK = WINDOW
ATTN_WIDTH = ATTN_HEADS * ATTN_HEAD_DIM
KV_WIDTH = ATTN_KV_HEADS * ATTN_HEAD_DIM
HGRN_HEADS = 8
HGRN_KEY_DIM = 128
HGRN_VALUE_DIM = 128
HGRN_WIDTH = HGRN_HEADS * HGRN_VALUE_DIM
HGRN_CHUNK = 64
REL_BUCKETS = 32
REL_MAX_DIST = 128
NORM_EPS = 1e-6
IN_WIDTHS = (ATTN_WIDTH, KV_WIDTH, KV_WIDTH, ATTN_WIDTH,
             HGRN_HEADS * HGRN_KEY_DIM, HGRN_HEADS * HGRN_KEY_DIM, HGRN_WIDTH, HGRN_WIDTH,
             D_MODEL, D_MODEL)
IN_PROJ_WIDTH = 10752

kernel_name = 'hybrid_swa_sinks_hgrn2_gated_merge'


def rms_norm(x, gain):
    xf = x.astype(jnp.float32)
    y = xf * lax.rsqrt(jnp.mean(xf * xf, axis=-1, keepdims=True) + NORM_EPS)
    return (y * gain.astype(jnp.float32)).astype(x.dtype)


def t5_bucket(dist):
    max_exact = REL_BUCKETS // 2
    d = jnp.maximum(dist, 0)
    df = jnp.maximum(d, 1).astype(jnp.float32)
    large = max_exact + (jnp.log(df / max_exact) / math.log(REL_MAX_DIST / max_exact)
                         * (REL_BUCKETS - max_exact)).astype(jnp.int32)
    large = jnp.minimum(large, REL_BUCKETS - 1)
    return jnp.where(d < max_exact, d, large)


def sliding_window_attention(q, k, v, sinks, rel_bias):
    B, S, _ = q.shape
    nb = S // ATTN_BLOCK
    qb = q.astype(jnp.float32).reshape(B, nb, ATTN_BLOCK, ATTN_KV_HEADS, ATTN_GROUP, ATTN_HEAD_DIM)

    def windows(t):
        t = t.astype(jnp.float32).reshape(B, S, ATTN_KV_HEADS, ATTN_HEAD_DIM)
        t = jnp.pad(t, ((0, 0), (ATTN_BLOCK, 0), (0, 0), (0, 0)))
        t = t.reshape(B, nb + 1, ATTN_BLOCK, ATTN_KV_HEADS, ATTN_HEAD_DIM)
        return jnp.concatenate([t[:, :-1], t[:, 1:]], axis=2)

    kw, vw = windows(k), windows(v)
    scores = jnp.einsum('bnqhgd,bnshd->bhgnqs', qb, kw) * (ATTN_HEAD_DIM ** -0.5)
    qi = jnp.arange(ATTN_BLOCK)[:, None]
    si = jnp.arange(2 * ATTN_BLOCK)[None, :]
    dist = qi + ATTN_BLOCK - si
    band = (dist >= 0) & (dist < WINDOW)
    key_pos = jnp.arange(nb)[:, None] * ATTN_BLOCK - ATTN_BLOCK + jnp.arange(2 * ATTN_BLOCK)[None, :]
    mask = band[None] & (key_pos >= 0)[:, None, :]
    bias = rel_bias[t5_bucket(dist)].astype(jnp.float32)
    bias = jnp.transpose(bias, (2, 0, 1)).reshape(ATTN_KV_HEADS, ATTN_GROUP, 1, ATTN_BLOCK, 2 * ATTN_BLOCK)
    scores = jnp.where(mask, scores + bias, -jnp.inf)
    sink = jnp.broadcast_to(sinks.astype(jnp.float32).reshape(ATTN_KV_HEADS, ATTN_GROUP, 1, 1, 1),
                            scores.shape[:-1] + (1,))
    probs = jax.nn.softmax(jnp.concatenate([scores, sink], axis=-1), axis=-1)[..., :-1]
    out = jnp.einsum('bhgnqs,bnshd->bnqhgd', probs, vw)
    return out.reshape(B, S, ATTN_WIDTH)


def hgrn2_recurrence(q, f_pre, i, lb):
    B, S, _ = q.shape
    nc = S // HGRN_CHUNK

    def chunks(t):
        return t.reshape(B, nc, HGRN_CHUNK, HGRN_HEADS, -1).transpose(1, 0, 3, 2, 4)

    lbf = lb.astype(jnp.float32)
    f = lbf + (1.0 - lbf) * jax.nn.sigmoid(f_pre.astype(jnp.float32))
    qc = chunks(jax.nn.silu(q.astype(jnp.float32)))
    kc = chunks(1.0 - f)
    vc = chunks(i.astype(jnp.float32))
    gc = jnp.cumsum(chunks(jnp.log(f)), axis=3)
    causal = jnp.tril(jnp.ones((HGRN_CHUNK, HGRN_CHUNK), dtype=bool))

    def step(state, inp):
        qt, kt, vt, gt = inp
        inter = jnp.einsum('bhcd,bhde->bhce', qt * jnp.exp(gt), state)
        diff = gt[:, :, :, None, :] - gt[:, :, None, :, :]
        decay = jnp.exp(jnp.where(causal[:, :, None], diff, -jnp.inf))
        attn = jnp.einsum('bhtd,bhsd,bhtsd->bhts', qt, kt, decay)
        intra = jnp.einsum('bhts,bhse->bhte', attn, vt)
        g_last = gt[:, :, -1]
        k_dec = kt * jnp.exp(g_last[:, :, None, :] - gt)
        new_state = jnp.exp(g_last)[..., None] * state + jnp.einsum('bhsd,bhse->bhde', k_dec, vt)
        return new_state, inter + intra

    s0 = jnp.zeros((B, HGRN_HEADS, HGRN_KEY_DIM, HGRN_VALUE_DIM), jnp.float32)
    _, o = lax.scan(step, s0, (qc, kc, vc, gc))
    return o.transpose(1, 0, 3, 2, 4).reshape(B, S, HGRN_HEADS, HGRN_VALUE_DIM)


def setup_inputs(seed: int = 0) -> dict:
    key = jax.random.key(seed)
    ks = jax.random.split(key, 12)
    f32 = jnp.float32
    return {
        'x': jax.random.normal(ks[0], (BATCH, SEQ, D_MODEL), f32),
        'norm_pre': 1.0 + 0.1 * jax.random.normal(ks[1], (DEPTH, D_MODEL), f32),
        'w_in': jax.random.normal(ks[2], (DEPTH, D_MODEL, IN_PROJ_WIDTH), f32) * D_MODEL ** -0.5,
        'rel_bias': 0.5 * jax.random.normal(ks[3], (REL_BUCKETS, ATTN_HEADS), f32),
        'attn_sinks': jax.random.normal(ks[4], (DEPTH, ATTN_HEADS), f32),
        'lb_logits': 0.5 * jax.random.normal(ks[5], (DEPTH + 1, HGRN_HEADS * HGRN_KEY_DIM), f32),
        'hgrn_norm': 1.0 + 0.1 * jax.random.normal(ks[6], (DEPTH, HGRN_HEADS, HGRN_VALUE_DIM), f32),
        'w_branch_attn': jax.random.normal(ks[7], (DEPTH, ATTN_WIDTH, D_MODEL), f32) * ATTN_WIDTH ** -0.5,
        'w_branch_hgrn': jax.random.normal(ks[8], (DEPTH, HGRN_WIDTH, D_MODEL), f32) * HGRN_WIDTH ** -0.5,
        'w_out': jax.random.normal(ks[9], (DEPTH, D_MODEL, D_MODEL), f32) * D_MODEL ** -0.5,
        'norm_post': 1.0 + 0.1 * jax.random.normal(ks[10], (DEPTH, D_MODEL), f32),
    }


def reference(x, norm_pre, w_in, rel_bias, attn_sinks, lb_logits, hgrn_norm,
              w_branch_attn, w_branch_hgrn, w_out, norm_post):
    split_points = [int(p) for p in np.cumsum(IN_WIDTHS)[:-1]]
    lower_bounds = jnp.cumsum(jax.nn.softmax(lb_logits.astype(jnp.float32), axis=0), axis=0)[:DEPTH]
    for layer in range(DEPTH):
        h = rms_norm(x, norm_pre[layer])
        proj = jnp.matmul(h, w_in[layer])
        aq, ak, av, ag, hq, hf, hi, hg, gate_a, gate_h = jnp.split(proj, split_points, axis=-1)
        ya = sliding_window_attention(aq, ak, av, attn_sinks[layer], rel_bias)
        ya = (ya * jax.nn.silu(ag.astype(jnp.float32))).astype(x.dtype)
        oh = rms_norm(hgrn2_recurrence(hq, hf, hi, lower_bounds[layer]), hgrn_norm[layer])
        yh = (oh.reshape(oh.shape[0], oh.shape[1], HGRN_WIDTH)
              * jax.nn.silu(hg.astype(jnp.float32))).astype(x.dtype)
        ua = jnp.matmul(ya, w_branch_attn[layer])
        uh = jnp.matmul(yh, w_branch_hgrn[layer])
        merged = jax.nn.sigmoid(gate_a) * ua + jax.nn.sigmoid(gate_h) * uh
        y = jnp.matmul(merged, w_out[layer])
        x = x + rms_norm(y, norm_post[layer]).astype(x.dtype)
    return x
```

```python
import contextlib
import numpy as np
import ml_dtypes
import concourse.bass as bass
import concourse.mybir as mybir
from concourse.bass_utils import run_bass_kernel_spmd

F32 = mybir.dt.float32
BF16 = mybir.dt.bfloat16
AF = mybir.ActivationFunctionType
ALU = mybir.AluOpType

D = 2048
NKC = 16
TT = 512
EPS = 1e-6
AQ0, AK0, AV0, AG0, HQ0, HF0, HI0, HG0, GA0, GH0 = 0, 1024, 1280, 1536, 2560, 3584, 4608, 5632, 6656, 8704
IN_W = 10752
NW = 2
STOP = None
FUSE0 = True
FUSEALL = False


class Tok:
    __slots__ = ("sem", "val")

    def __init__(self, sem, val):
        self.sem = sem
        self.val = val


class Buf:
    __slots__ = ("name", "w", "r", "excl")

    def __init__(self, name, excl=False):
        self.name = name
        self.w = None
        self.r = {}
        self.excl = excl


class DSem:
    def __init__(self, sem):
        self.sem = sem
        self.cnt = 0
        self.open = []


class Trk:
    def __init__(self, nc, es):
        self.nc = nc
        self.es = es
        self.eng = {"pe": nc.tensor, "act": nc.scalar, "dve": nc.vector, "pool": nc.gpsimd, "sp": nc.sync}
        self.sem = {e: es.enter_context(nc.semaphore("s_" + e)) for e in ("pe", "act", "dve", "pool")}
        self.cnt = {e: 0 for e in self.sem}
        self.waited = {e: {} for e in self.eng}
        self.pending_pe = []
        self.dsems = []

    def dsem(self, name):
        d = DSem(self.es.enter_context(self.nc.semaphore(name)))
        self.dsems.append(d)
        return d

    def _wait(self, e, r, w):
        need = {}

        def add(t):
            if t is None:
                return
            if e == "pe" and t.sem is self.sem["pe"]:
                return
            assert t.val is not None, "pending PE token used as dependency"
            k = id(t.sem)
            if k not in need or need[k][1] < t.val:
                need[k] = (t.sem, t.val)

        for b in r:
            add(b.w)
        for b in w:
            add(b.w)
            for t in b.r.values():
                add(t)
        for k, (sem, val) in need.items():
            if self.waited[e].get(k, 0) >= val:
                continue
            self.eng[e].wait_ge(sem, val)
            self.waited[e][k] = val

    def _record(self, r, w, tok):
        for b in r:
            b.r[id(tok.sem)] = tok
        for b in w:
            b.w = tok
            b.r = {}

    def op(self, e, fn, r=(), w=(), signal=True):
        w = list(w) + [b for b in r if b.excl]
        r = [b for b in r if not b.excl]
        self._wait(e, r, w)
        inst = fn()
        if e == "pe" and not signal:
            tok = Tok(self.sem["pe"], None)
            self.pending_pe.append(tok)
        else:
            self.cnt[e] += 1
            inst.then_inc(self.sem[e], 1)
            tok = Tok(self.sem[e], self.cnt[e])
            if e == "pe":
                for p in self.pending_pe:
                    p.val = self.cnt[e]
                self.pending_pe = []
        self._record(r, w, tok)
        return inst

    def dma(self, q, out, in_, dsem, r=(), w=()):
        self._wait(q, r, w)
        inst = self.eng[q].dma_start(out=out, in_=in_)
        dsem.cnt += 16
        inst.then_inc(dsem.sem, 16)
        tok = Tok(dsem.sem, dsem.cnt)
        dsem.open.append(tok)
        self._record(r, w, tok)
        return inst

    def close_batch(self, dsem):
        for t in dsem.open:
            t.val = dsem.cnt
        dsem.open = []

    def inherit(self, dst, src):
        for d in dst:
            for b in src:
                toks = list(b.r.values()) + ([b.w] if b.w is not None else [])
                for t in toks:
                    k = id(t.sem)
                    if t.val is None:
                        d.r[("p", id(t))] = t
                    elif k not in d.r or d.r[k].val is None or d.r[k].val < t.val:
                        d.r[k] = t

    def barrier(self, engines=("pe", "act", "dve", "pool", "sp")):
        assert not self.pending_pe
        for e in engines:
            for s in ("pe", "act", "dve", "pool"):
                if s == e and e == "pe":
                    continue
                v = self.cnt[s]
                k = id(self.sem[s])
                if v > 0 and self.waited[e].get(k, 0) < v:
                    self.eng[e].wait_ge(self.sem[s], v)
                    self.waited[e][k] = v
            for d in self.dsems:
                k = id(d.sem)
                if d.cnt > 0 and self.waited[e].get(k, 0) < d.cnt:
                    self.eng[e].wait_ge(d.sem, d.cnt)
                    self.waited[e][k] = d.cnt


def make_plan():
    units = {}

    def U(name, src, kc, col0, dup=False):
        units[name] = dict(src=src, kc=kc, col0=col0, dup=dup)

    for i in range(2):
        U(f"AV{i}", "win", 16, AV0 + 128 * i)
    for g in range(4):
        U(f"KD{g}", "win", 16, AK0 + 64 * g, dup=True)
        for j in range(2):
            U(f"Q{g}{j}", "win", 16, AQ0 + 64 * (4 * g + 2 * j))
            U(f"AG{g}{j}", "win", 16, AG0 + 64 * (4 * g + 2 * j))
    for h in range(8):
        U(f"HI{h}", "win", 16, HI0 + 128 * h)
        U(f"HQ{h}", "win", 16, HQ0 + 128 * h)
        U(f"HF{h}", "win", 16, HF0 + 128 * h)
        U(f"HG{h}", "win", 16, HG0 + 128 * h)
    for c in range(16):
        U(f"GA{c}", "win", 16, GA0 + 128 * c)
        U(f"GH{c}", "win", 16, GH0 + 128 * c)
        U(f"WA{c}", "wa", 8, 128 * c)
        U(f"WH{c}", "wh", 8, 128 * c)
        U(f"WO{c}", "wo", 16, 128 * c)

    groups = [["AV0", "AV1", "KD0", "KD1"], [f"HI{h}" for h in range(4)], [f"HI{h}" for h in range(4, 8)]]
    groups.append(["KD2", "KD3"])
    flat = []
    for g in range(4):
        flat += [f"Q{g}0", f"Q{g}1", f"AG{g}0", f"AG{g}1"]
    for h in range(8):
        flat += [f"HQ{h}", f"HF{h}", f"HG{h}"]
    for i in range(0, len(flat), 4):
        groups.append(flat[i:i + 4])
    for j in range(8):
        groups.append([f"GA{2 * j}", f"WA{2 * j}", f"GA{2 * j + 1}", f"WA{2 * j + 1}"])
    for j in range(8):
        groups.append([f"GH{2 * j}", f"WH{2 * j}", f"GH{2 * j + 1}", f"WH{2 * j + 1}"])
    for i in range(4):
        groups.append([f"WO{4 * i + k}" for k in range(4)])
    gsize = []
    for gi, g in enumerate(groups):
        off = 0
        for n in g:
            units[n]["g"] = gi
            units[n]["off"] = off
            off += units[n]["kc"]
        assert off <= 64
        gsize.append(off)

    jobs = []
    jobs.append(("win", 16, AV0, 256, ["AV0", "AV1"]))
    jobs.append(("win", 16, AK0, 256, ["KD0", "KD1", "KD2", "KD3"]))
    for i in range(2):
        jobs.append(("win", 16, HI0 + 512 * i, 512, [f"HI{4 * i + k}" for k in range(4)]))
    for g in range(4):
        jobs.append(("win", 16, AQ0 + 256 * g, 256, [f"Q{g}0", f"Q{g}1"]))
        jobs.append(("win", 16, AG0 + 256 * g, 256, [f"AG{g}0", f"AG{g}1"]))
    for nm, c0 in (("HQ", HQ0), ("HF", HF0), ("HG", HG0)):
        for i in range(2):
            jobs.append(("win", 16, c0 + 512 * i, 512, [f"{nm}{4 * i + k}" for k in range(4)]))
    for nm, src, kc, c0 in (("GA", "win", 16, GA0), ("GH", "win", 16, GH0), ("WA", "wa", 8, 0), ("WH", "wh", 8, 0),
                            ("WO", "wo", 16, 0)):
        for i in range(4):
            jobs.append((src, kc, c0 + 512 * i, 512, [f"{nm}{4 * i + k}" for k in range(4)]))
    jobs.sort(key=lambda j: min(units[n]["g"] for n in j[4]))
    return units, groups, gsize, jobs


def build_nc(NSEQ, S):
    assert S % TT == 0
    NT = S // TT
    NTOK = NSEQ * S
    units, groups, gsize, jobs = make_plan()
    NG = len(groups)

    nc = bass.Bass("TRN2", target_bir_lowering=False)

    def din(name, shape, dt=F32):
        return nc.dram_tensor(name, shape, dt, kind="ExternalInput").ap()

    x = din("x", [NTOK, D])
    wsrc = {"win": din("w_in", [D, IN_W]), "wa": din("w_a", [1024, D]), "wh": din("w_h", [1024, D]),
            "wo": din("w_o", [D, D])}
    d_npre = din("npre", [128, 16])
    d_npost = din("npost", [128, D])
    d_biasg = din("biasg", [128, 4096])
    d_amask = din("amask", [128, 4096])
    d_sinks = din("sinks", [128, 16])
    d_lbl = din("lbl", [128, 16])
    d_hgain = din("hgain", [128, 8])
    d_ident = din("ident", [128, 128], BF16)
    d_hmask = din("hmask", [128, 128], BF16)
    out = nc.dram_tensor("out", [NTOK, D], F32, kind="ExternalOutput").ap()
    wsc = nc.dram_tensor("wsc", [NG, 128, 8192], BF16, kind="Internal").ap()

    with contextlib.ExitStack() as es:
        T = Trk(nc, es)
        V, A, G, PE = nc.vector, nc.scalar, nc.gpsimd, nc.tensor

        def sb(name, shape, dt, stack=es):
            return stack.enter_context(nc.sbuf_tensor("sb_" + name, shape, dt))

        def psum(name, shape, dt):
            return es.enter_context(nc.psum_tensor("ps_" + name, shape, dt))

        ident = sb("ident", [128, 128], BF16)
        hmask = sb("hmask", [128, 128], BF16)
        ones = sb("ones", [128, 128], BF16)
        zeros = sb("zeros", [128, 64], F32)
        mhalf = sb("mhalf", [128, 1], F32)
        npre = sb("npre", [128, 16], F32)
        npost = sb("npost", [128, D], F32)
        Emb = sb("Emb", [128, 4, 1024], BF16)
        es2 = sb("es2", [128, 16], F32)
        lbl = sb("lblt", [128, 16], F32)
        coef = sb("coef", [128, 4, 8], F32)
        hgain = sb("hgaint", [128, 8], F32)
        Bc = Buf("consts")
        dC = T.dsem("dC")

        T.dma("sp", ident[:], d_ident, dC, w=[Bc])
        T.dma("sp", hmask[:], d_hmask, dC, w=[Bc])
        T.dma("sp", npre[:], d_npre, dC, w=[Bc])
        T.dma("sp", npost[:], d_npost, dC, w=[Bc])
        T.dma("sp", es2[:], d_sinks, dC, w=[Bc])
        T.dma("sp", lbl[:], d_lbl, dC, w=[Bc])
        T.dma("sp", hgain[:], d_hgain, dC, w=[Bc])
        T.close_batch(dC)
        T.op("pool", lambda: G.memset(ones[:], 1.0), w=[Bc])
        T.op("pool", lambda: G.memset(zeros[:], 0.0), w=[Bc])
        T.op("pool", lambda: G.memset(mhalf[:], -0.5), w=[Bc])

        with contextlib.ExitStack() as ps_:
            NST = 3 if not FUSE0 else 2
            st32 = [sb(f"st32_{i}", [128, 16 * 512], F32, ps_) for i in range(NST)]
            st16 = [sb(f"st16_{i}", [128, 4, 2048], BF16, ps_) for i in range(NST if not FUSE0 else 0)]
            etmp = st32[0][:, 0:4096]
            mtmp = st32[1][:, 0:4096]
            B32 = [Buf(f"st32_{i}") for i in range(NST)]
            B16 = [Buf(f"st16_{i}") for i in range(NST)]
            d32 = [T.dsem(f"d32_{i}") for i in range(NST)]
            d16 = [T.dsem(f"d16_{i}") for i in range(NST)]
            Bet = Buf("etmp")

            dC2 = T.dsem("dC2")
            T.dma("sp", etmp, d_biasg, dC2, w=[Bet, B32[0]])
            T.dma("sp", mtmp, d_amask, dC2, w=[Bet, B32[1]])
            T.close_batch(dC2)
            T.op("act", lambda: A.activation(out=etmp, in_=etmp, func=AF.Exp), r=[Bet], w=[Bet, B32[0]])
            T.op("dve", lambda: V.tensor_tensor(out=Emb[:].rearrange("p g c -> p (g c)"), in0=etmp, in1=mtmp,
                                                op=ALU.mult), r=[Bet, B32[0], B32[1]], w=[Bc])
            T.op("act", lambda: A.activation(out=es2[:], in_=es2[:], func=AF.Exp), r=[Bc], w=[Bc])
            T.op("dve", lambda: V.tensor_scalar(out=es2[:], in0=es2[:], scalar1=2.0, scalar2=None, op0=ALU.mult),
                 r=[Bc], w=[Bc])
            T.op("dve", lambda: V.tensor_tensor(out=lbl[:, 0:8], in0=lbl[:, 0:8], in1=lbl[:, 8:16], op=ALU.subtract),
                 r=[Bc], w=[Bc])
            T.op("act", lambda: A.activation(out=lbl[:, 0:8], in_=lbl[:, 0:8], func=AF.Tanh, scale=0.5), r=[Bc], w=[Bc])
            T.op("dve", lambda: V.tensor_scalar(out=coef[:, 0, :], in0=lbl[:, 0:8], scalar1=0.25, scalar2=0.75,
                                                op0=ALU.mult, op1=ALU.add), r=[Bc], w=[Bc])
            T.op("dve", lambda: V.tensor_scalar(out=coef[:, 1, :], in0=lbl[:, 0:8], scalar1=-0.25, scalar2=0.25,
                                                op0=ALU.mult, op1=ALU.add), r=[Bc], w=[Bc])
            T.op("dve", lambda: V.tensor_scalar(out=coef[:, 2, :], in0=lbl[:, 0:8], scalar1=0.25, scalar2=-0.25,
                                                op0=ALU.mult, op1=ALU.add), r=[Bc], w=[Bc])
            T.op("dve", lambda: V.tensor_scalar(out=coef[:, 3, :], in0=hgain[:], scalar1=0.5, scalar2=None,
                                                op0=ALU.mult), r=[Bc], w=[Bc])

            if not FUSE0:
                def job_load(ji):
                    if ji >= len(jobs):
                        return
                    src, kc, col0, ncols, names = jobs[ji]
                    s_ = ji % NST
                    sv = st32[s_][:, 0:kc * ncols].rearrange("p (k c) -> p k c", c=ncols)
                    T.dma("sp", sv, wsrc[src].rearrange("(k p) c -> p k c", p=128)[:, :, col0:col0 + ncols], d32[s_],
                          w=[B32[s_]])

                for ji in range(NST):
                    job_load(ji)
                for ji, (src, kc, col0, ncols, names) in enumerate(jobs):
                    s = ji % NST
                    sv = st32[s][:, 0:kc * ncols].rearrange("p (k c) -> p k c", c=ncols)
                    for ui, n in enumerate(names):
                        u = units[n]
                        ov = st16[s][:, ui, 0:kc * 128].rearrange("p (k c) -> p k c", c=128)
                        eng = "act" if (ji + ui) % 2 == 0 else "dve"
                        cp = (lambda o, i: A.copy(o, i)) if eng == "act" else (lambda o, i: V.tensor_copy(o, i))
                        if u["dup"]:
                            iv = sv[:, :, 64 * ui:64 * ui + 64]
                            T.op(eng, lambda: cp(ov[:, :, 0:64], iv), r=[B32[s]], w=[B16[s]])
                            T.op(eng, lambda: cp(ov[:, :, 64:128], iv), r=[B32[s]], w=[B16[s]])
                        else:
                            iv = sv[:, :, 128 * ui:128 * ui + 128]
                            T.op(eng, lambda: cp(ov, iv), r=[B32[s]], w=[B16[s]])
                    for ui, n in enumerate(names):
                        u = units[n]
                        T.dma("sp", wsc[u["g"], :, u["off"] * 128:(u["off"] + kc) * 128], st16[s][:, ui, 0:kc * 128],
                              d16[s], r=[B16[s]])
                    job_load(ji + NST)
            T.barrier()

        arA = sb("arA", [128, 16384], BF16)
        hT = arA[:, 0:8192].rearrange("p (k t) -> p k t", t=TT)
        yaT = arA[:, 8192:12288].rearrange("p (k t) -> p k t", t=TT)
        yhT = arA[:, 12288:16384].rearrange("p (k t) -> p k t", t=TT)
        Qs2 = yhT
        BhT, ByaT, ByhT = Buf("hT"), Buf("yaT"), Buf("yhT")

        arB = sb("arB", [128, 8192], BF16)
        thf = arB[:].bitcast(F32).rearrange("p (h t) -> p h t", t=TT)
        mergedT = arB[:].rearrange("p (k t) -> p k t", t=TT)
        BarB = Buf("arB")

        arC = sb("arC", [128, 4096], BF16)
        hb = arC[:].rearrange("p (s c) -> p s c", c=D)
        tmpo = arC[:].bitcast(F32)
        BarC = [Buf("arC0"), Buf("arC1")]

        arDE = sb("arDE", [128, 16384], BF16)
        arD = arDE[:, 0:9216]
        ysb = arDE[:].bitcast(F32).rearrange("p (b c) -> p b c", c=D)
        Bysb = Buf("ysb")
        QT = arD[:, 0:2048].rearrange("p (s j t) -> p s j t", s=2, j=2)
        AGs = arD[:, 2048:4096].rearrange("p (s j t) -> p s j t", s=2, j=2)
        Pe = arD[:, 4096:6144].rearrange("p (s c) -> p s c", s=2)
        PT = arD[:, 6144:8192].rearrange("p (s c) -> p s c", s=2)
        yatm = arD[:, 8192:8704].rearrange("p (s c) -> p s c", s=2)
        KaT = arD[:, 0:4096].rearrange("p (h t) -> p h t", t=TT)
        KdT = arD[:, 4096:8192].rearrange("p (h t) -> p h t", t=TT)
        BarD = Buf("arD")
        BQT = [Buf("QT0"), Buf("QT1")]
        BAGs = [Buf("AGs0"), Buf("AGs1")]
        BPe = [Buf("Pe0"), Buf("Pe1")]
        BPT = [Buf("PT0"), Buf("PT1")]
        BPT2 = [Buf("PT0b"), Buf("PT1b")]
        Byatm = [Buf("yatm0"), Buf("yatm1")]
        Batt = BQT + BAGs + BPe + BPT + BPT2 + Byatm

        arE = arDE[:, 9216:15360]
        arE32 = arE.bitcast(F32)
        BarE = Buf("arE")
        BH2 = [Buf(f"h2_{n}") for n in ("lnf", "Gc", "eG", "erG", "k1", "kk")]
        Bsq = [Buf(f"sq{i}") for i in range(4)]
        Blr = [Buf(f"lr{i}") for i in range(4)]
        Bot = [Buf(f"ot{i}") for i in range(4)]
        Btha = [Buf("tha0"), Buf("tha1")]
        grpDE = [BarD, BarE, Bysb] + Batt + BH2 + Bsq + Blr + Bot + Btha

        def enter(bufs):
            T.inherit(bufs, [b_ for b_ in grpDE if all(b_ is not q_ for q_ in bufs)])

        ss2 = sb("ss2", [128, 4], F32)
        rs2 = sb("rs2", [128, 4], F32)
        Bss2 = [Buf(f"ss2{b}") for b in range(4)]
        Brs2 = [Buf(f"rs2{b}") for b in range(4)]

        Wb = [sb(f"Wb{i}", [128, 8192], BF16) for i in range(NW)]
        BW = [[Buf(f"W{i}_{u}") for u in range(8)] for i in range(NW)]

        def wregs(s_, off, kc):
            return BW[s_][off // 8:(off + kc) // 8]
        dW = [T.dsem(f"dW{i}") for i in range(NW)]
        xa = sb("xa", [128, 2, D], F32)
        Bxa = [Buf("xa0"), Buf("xa1")]
        dxl = [T.dsem("dxl0"), T.dsem("dxl1")]
        dxs = [T.dsem("dxs0"), T.dsem("dxs1")]
        dxl2 = T.dsem("dxl2")
        dxs2 = T.dsem("dxs2")
        KT = sb("KT", [128, 4, 640], BF16)
        BKT = Buf("KT")
        Vaug = sb("Vaug", [128, 5, 4, 66], BF16)
        BVa = Buf("Vaug")
        Vh = sb("Vh", [128, 4, 1024], BF16)
        BVh = Buf("Vh")
        Gs2 = sb("Gs2", [128, 8, TT], BF16)
        BGs2 = Buf("Gs2")
        QaT = sb("QaT", [128, 8, TT], BF16)
        BQaT = Buf("QaT")
        Plast = sb("Plast", [128, 8, 8], F32)
        BPl = Buf("Plast")
        S32 = sb("S32", [128, 8, 128], F32)
        Sbf = sb("Sbf", [128, 8, 2, 128], BF16)
        BS32 = [Buf(f"S32_{h}") for h in range(8)]
        BSb = [[Buf(f"Sb{h}_{i}") for i in range(2)] for h in range(8)]
        Kdtm = sb("Kdtm", [128, 8, 128], BF16)
        BKdtm = [Buf(f"Kdtm{h}") for h in range(8)]
        Am = sb("Am", [128, 8, 128], BF16)
        BAm = [Buf(f"Am{h}") for h in range(8)]
        tht = sb("tht", [128, 2, TT], F32)
        Btht = [Buf("tht0"), Buf("tht1")]
        ss = sb("ss", [128, 4], F32)
        rs = sb("rs", [128, 4], F32)
        Bss = [Buf(f"ss{b}") for b in range(4)]
        Brs = [Buf(f"rs{b}") for b in range(4)]
        ssq = sb("ssq", [128, 4, 4], F32)
        Bssq = [Buf(f"ssq{b}") for b in range(4)]
        dd = sb("dd", [128, 2, 4], F32)
        rd = sb("rd", [128, 2, 4], F32)
        Bdd = [Buf("dd0"), Buf("dd1")]
        junk = sb("junk", [128, TT], BF16)
        Bjunk = Buf("junk")

        pb = [psum(f"pb{i}", [128, 512], F32) for i in range(8)]
        Bpb = [Buf(f"pb{i}", excl=True) for i in range(8)]
        ptbs = {i: pb[i][:].bitcast(BF16).rearrange("p (s c) -> p s c", c=128) for i in (6, 7)}
        ptb = ptbs[7]

        T.op("pool", lambda: G.memset(Vaug[:], 2.0), w=[BVa])

        gseq = [(t, g) for t in range(NSEQ * NT) for g in range(NG)]
        wstate = {"loaded": 0, "cur": 0}

        Bwsc = [Buf(f"wsc{g}") for g in range(NG)]
        dWs = [T.dsem(f"dWs{i}") for i in range(NW)]
        stg = [xa[:, 0, :], xa[:, 1, :], arC[:].bitcast(F32)]
        Bstg = [[Bxa[0]], [Bxa[1]], BarC]
        dstg = [dxl[0], dxl[1], dxl2]
        cvt = [0]

        def convert_group(g, s):
            names = groups[g]
            slots = []

            def load(n_):
                u = units[n_]
                kc = u["kc"]
                k_ = cvt[0] % 3
                cvt[0] += 1
                src = wsrc[u["src"]].rearrange("(k p) c -> p k c", p=128)
                w_ = 64 if u["dup"] else 128
                sv = stg[k_][:, 0:kc * w_].rearrange("p (k c) -> p k c", c=w_)
                T.dma("sp", sv, src[:, :, u["col0"]:u["col0"] + w_], dstg[k_], w=Bstg[k_])
                slots.append((sv, k_))

            def cast(ui, n_):
                u = units[n_]
                kc = u["kc"]
                sv, k_ = slots[ui]
                ov = Wb[s][:, u["off"] * 128:(u["off"] + kc) * 128].rearrange("p (k c) -> p k c", c=128)
                eng = ("act", "dve")[ui % 2]
                cp = (lambda o, i_: A.copy(o, i_)) if eng == "act" else (lambda o, i_: V.tensor_copy(o, i_))
                bwu = wregs(s, u["off"], kc)
                if u["dup"]:
                    T.op(eng, lambda: cp(ov[:, :, 0:64], sv), r=Bstg[k_], w=bwu)
                    T.op(eng, lambda: cp(ov[:, :, 64:128], sv), r=Bstg[k_], w=bwu)
                else:
                    T.op(eng, lambda: cp(ov, sv), r=Bstg[k_], w=bwu)

            for n_ in names[:3]:
                load(n_)
            for ui, n_ in enumerate(names):
                cast(ui, n_)
                if ui + 3 < len(names):
                    load(names[ui + 3])
            n = gsize[g] * 128
            T.dma("sp", wsc[g, :, 0:n], Wb[s][:, 0:n], dWs[s], r=BW[s], w=[Bwsc[g]])

        def emit_load(i):
            if i >= len(gseq):
                return
            t_, g = gseq[i]
            s = i % NW
            n = gsize[g] * 128
            if FUSE0 and (t_ == 0 or FUSEALL):
                convert_group(g, s)
            else:
                T.dma("sp", Wb[s][:, 0:n], wsc[g, :, 0:n], dW[s], r=[Bwsc[g]], w=BW[s])
            wstate["loaded"] = i + 1

        def wslot(name):
            g = units[name]["g"]
            i = wstate["cur"]
            assert gseq[i][1] == g, (name, g, gseq[i])
            return i % NW

        def wunit(name):
            u = units[name]
            s = wslot(name)
            v = Wb[s][:, u["off"] * 128:(u["off"] + u["kc"]) * 128].rearrange("p (k c) -> p k c", c=128)
            return v, wregs(s, u["off"], u["kc"])

        def wmov(names):
            u0 = units[names[0]]
            s = wslot(names[0])
            nu = len(names)
            v = Wb[s][:, u0["off"] * 128:(u0["off"] + nu * 16) * 128].rearrange("p (u k c) -> p u k c", u=nu, k=16)
            return v, wregs(s, u0["off"], nu * 16)

        def wdone():
            i = wstate["cur"]
            wstate["cur"] = i + 1
            emit_load(i + NW)

        pcount = [0]

        def proj_bank():
            b = pcount[0] % 2
            pcount[0] += 1
            return b

        def mm(o, l, r_, start, stop, r, w, signal=True):
            return T.op("pe", lambda: PE.matmul(o, l, r_, start=start, stop=stop), r=r, w=w, signal=signal)

        def proj_fm(name, evac):
            wv, bw = wunit(name)
            bk = proj_bank()
            for k in range(16):
                mm(pb[bk][:], wv[:, k, :], hT[:, k, :], k == 0, k == 15, bw + [BhT], [Bpb[bk]], signal=(k == 15))
            evac(pb[bk], Bpb[bk])

        xcnt = [0]
        thc = [0]

        def tanh_evac(pbk, bpbk):
            s = thc[0] % 2
            thc[0] += 1
            T.op("act", lambda: A.activation(out=tht[:, s, :], in_=pbk[:], func=AF.Tanh, scale=0.5), r=[bpbk],
                 w=[Btht[s]])
            return tht[:, s, :], Btht[s]

        xslot = {}
        deferred = []

        def phase_x_pre(t, b):
            seq_, ti_ = divmod(t, NT)
            r0 = seq_ * S + ti_ * TT + 128 * b
            s = b % 2
            xslot[b] = s
            T.dma("sp", xa[:, s, :], x[r0:r0 + 128, :], dxl[s], w=[Bxa[s]])
            T.op("act", lambda: A.activation(out=hb[:, s, :], in_=xa[:, s, :], func=AF.Square,
                                             accum_out=ss[:, b:b + 1]), r=[Bxa[s]], w=[BarC[s], Bss[b]])
            T.op("pool", lambda: G.tensor_scalar(out=rs[:, b:b + 1], in0=ss[:, b:b + 1], scalar1=1.0 / D,
                                                 scalar2=EPS, op0=ALU.mult, op1=ALU.add), r=[Bss[b]], w=[Brs[b]])
            T.op("pool", lambda: G.tensor_tensor(out=rs[:, b:b + 1], in0=rs[:, b:b + 1], in1=mhalf[:],
                                                 op=ALU.pow), r=[Brs[b], Bc], w=[Brs[b]])
            T.op("dve", lambda: V.tensor_scalar(out=hb[:, s, :], in0=xa[:, s, :], scalar1=rs[:, b:b + 1],
                                                scalar2=None, op0=ALU.mult), r=[Bxa[s], Brs[b]], w=[BarC[s]])

        def phase_x_tr(b):
            s = xslot[b]
            for q4 in range(4):
                tb = 7 - (q4 % 2)
                for i in range(4):
                    kc = q4 * 4 + i
                    T.op("pe", lambda: PE.transpose(ptbs[tb][:, i, :], hb[:, s, kc * 128:(kc + 1) * 128],
                                                    ident[:]), r=[BarC[s], Bc], w=[Bpb[tb]], signal=(i == 3))
                T.op("dve", lambda: V.tensor_tensor(
                    out=hT[:, q4 * 4:(q4 + 1) * 4, b * 128:(b + 1) * 128], in0=ptbs[tb][:, 0:4, :],
                    in1=npre[:, q4 * 4:(q4 + 1) * 4].unsqueeze(2).broadcast_to([128, 4, 128]), op=ALU.mult),
                     r=[Bpb[tb], Bc], w=[BhT])

        def do_tile(t):
            if STOP == "prepass":
                return
            seq, ti = divmod(t, NT)
            tok0 = seq * S + ti * TT
            first = ti == 0

            if t == 0:
                for b in range(4):
                    phase_x_pre(t, b)
                    phase_x_tr(b)
                for i in range(NW):
                    emit_load(i)
            if STOP == "X":
                return
            wv, bw = wmov(["AV0", "AV1"])
            for b in range(4):
                bk = proj_bank()
                for k in range(16):
                    mm(pb[bk][:, 0:256].rearrange("p (u c) -> p u c", c=128), hT[:, k, b * 128:(b + 1) * 128],
                       wv[:, :, k, :], k == 0, k == 15, bw + [BhT], [Bpb[bk]], signal=(k == 15))
                T.op("act", lambda: A.copy(Vaug[:, 1 + b, :, 0:64],
                                           pb[bk][:, 0:256].rearrange("p (g c) -> p g c", c=64)), r=[Bpb[bk]], w=[BVa])

            def kd_evac(g):
                return lambda pbk, bpbk: T.op("act", lambda: A.copy(KT[:, g, 128:640], pbk[:]), r=[bpbk], w=[BKT])

            proj_fm("KD0", kd_evac(0))
            proj_fm("KD1", kd_evac(1))
            wdone()
            for half in range(2):
                names = [f"HI{4 * half + k}" for k in range(4)]
                wv, bw = wmov(names)
                for b in range(4):
                    bk = proj_bank()
                    for k in range(16):
                        mm(pb[bk][:].rearrange("p (u c) -> p u c", c=128), hT[:, k, b * 128:(b + 1) * 128],
                           wv[:, :, k, :], k == 0, k == 15, bw + [BhT], [Bpb[bk]], signal=(k == 15))
                    T.op("dve", lambda: V.tensor_copy(Vh[:, b, half * 512:(half + 1) * 512], pb[bk][:]),
                         r=[Bpb[bk]], w=[BVh])
                wdone()
            while deferred:
                deferred.pop(0)()
            gcount = [0]

            def unit_done():
                gcount[0] += 1
                if gcount[0] % 4 == 0:
                    wdone()

            enter(Batt)
            proj_fm("KD2", kd_evac(2))
            proj_fm("KD3", kd_evac(3))
            wdone()

            def q_unit(g, j):
                qs = g % 2
                proj_fm(f"Q{g}{j}", lambda pbk, bpbk: T.op(
                    "dve", lambda: V.tensor_copy(QT[:, qs, j, :], pbk[:]), r=[bpbk], w=[BQT[qs]]))
                unit_done()

            def ag_unit(g, j):
                qs = g % 2

                def ag_evac(pbk, bpbk):
                    th, bth = tanh_evac(pbk, bpbk)
                    T.op("dve", lambda: V.scalar_tensor_tensor(out=AGs[:, qs, j, :], in0=th, scalar=1.0, in1=pbk[:],
                                                               op0=ALU.add, op1=ALU.mult), r=[bth, bpbk],
                         w=[BAGs[qs]])
                proj_fm(f"AG{g}{j}", ag_evac)
                unit_done()

            def stage_a(g, n):
                qs = g % 2
                blks = [1] if (first and n == 0) else [0, 1]
                b0 = blks[0]
                qc = slice(n * 128, (n + 1) * 128)
                ps_ = n % 2
                sbk = (2, 3) if n % 2 == 0 else (5, 6)
                for p in range(2):
                    pr = slice(p * 64, (p + 1) * 64)
                    bk = sbk[p]
                    for bi, blk in enumerate(blks):
                        mm(pb[bk][:, blk * 256:(blk + 1) * 256].rearrange("p (j q) -> p j q", j=2),
                           KT[pr, g, (n + blk) * 128:(n + blk + 1) * 128], QT[pr, qs, :, qc], True, True,
                           [BKT, BQT[qs]], [Bpb[bk]], signal=(bi == len(blks) - 1))
                    T.op("act", lambda: A.activation(out=Pe[:, ps_, p * 512 + b0 * 256:(p + 1) * 512],
                                                     in_=pb[bk][:, b0 * 256:512], func=AF.Exp, scale=0.125),
                         r=[Bpb[bk]], w=[BPe[ps_]])
                for p in range(2):
                    cs_ = slice(p * 512 + b0 * 256, (p + 1) * 512)
                    if p == 0:
                        T.op("dve", lambda: V.tensor_tensor(out=PT[:, ps_, cs_], in0=Pe[:, ps_, cs_],
                                                            in1=Emb[:, g, cs_], op=ALU.mult),
                             r=[BPe[ps_], Bc], w=[BPT[ps_]])
                    else:
                        T.op("pool", lambda: G.tensor_tensor(out=PT[:, ps_, cs_], in0=Pe[:, ps_, cs_],
                                                             in1=Emb[:, g, cs_], op=ALU.mult),
                             r=[BPe[ps_], Bc], w=[BPT2[ps_]])

            def stage_b(g, n):
                blks = [1] if (first and n == 0) else [0, 1]
                ps_ = n % 2
                po = pb[4][:].rearrange("p (h c) -> p h c", c=128)
                ptv = PT[:, ps_, :].rearrange("p (a b j q) -> p a b j q", a=2, b=2, j=2)
                for j in range(2):
                    for p in range(2):
                        hs = 2 * j + p
                        for bi, blk in enumerate(blks):
                            last = (j == 1 and p == 1 and bi == len(blks) - 1)
                            mm(po[:, hs, 0:65], ptv[:, p, blk, j, :], Vaug[:, n + blk, g, 0:65], bi == 0,
                               bi == len(blks) - 1, [BPT[ps_], BPT2[ps_], BVa], [Bpb[4]], signal=last)
                T.op("dve", lambda: V.tensor_tensor(out=dd[:, ps_, :], in0=po[:, :, 64],
                                                    in1=es2[:, 4 * g:4 * g + 4], op=ALU.add),
                     r=[Bpb[4], Bc], w=[Bdd[ps_]])
                T.op("dve", lambda: V.reciprocal(rd[:, ps_, :], dd[:, ps_, :]), r=[Bdd[ps_]], w=[Bdd[ps_]])
                T.op("dve", lambda: V.tensor_tensor(
                    out=yatm[:, ps_, :].rearrange("p (h c) -> p h c", c=64), in0=po[:, :, 0:64],
                    in1=rd[:, ps_, :].unsqueeze(2).broadcast_to([128, 4, 64]), op=ALU.mult),
                     r=[Bpb[4], Bdd[ps_]], w=[Byatm[ps_]])

            def stage_c(g, n):
                qs = g % 2
                ps_ = n % 2
                qc = slice(n * 128, (n + 1) * 128)
                for j in range(2):
                    T.op("pe", lambda: PE.transpose(ptb[:, j, :], yatm[:, ps_, j * 128:(j + 1) * 128], ident[:]),
                         r=[Byatm[ps_], Bc], w=[Bpb[7]], signal=(j == 1))
                T.op("dve", lambda: V.tensor_tensor(out=yaT[:, 2 * g:2 * g + 2, qc], in0=ptb[:, 0:2, :],
                                                    in1=AGs[:, qs, :, qc], op=ALU.mult),
                     r=[Bpb[7], BAGs[qs]], w=[ByaT])

            lnf = arE32[:, 0:512]
            Gc = arE32[:, 512:1024]
            eG = arE32[:, 1024:1536]
            erG = arE32[:, 1536:2048]
            k1 = arE32[:, 2048:2560]
            kk = arE32[:, 2560:3072]
            Blnf, BGc, BeG, BerG, Bk1, Bkk = BH2

            def h2a(h):
                T.op("act", lambda: A.activation(out=lnf, in_=thf[:, h, :], func=AF.Ln, scale=coef[:, 1, h:h + 1],
                                                 bias=coef[:, 0, h:h + 1]), r=[BarB, Bc], w=[Blnf])
                T.op("pool", lambda: G.tensor_scalar(out=k1, in0=thf[:, h, :], scalar1=coef[:, 2, h:h + 1],
                                                     scalar2=coef[:, 1, h:h + 1], op0=ALU.mult, op1=ALU.add),
                     r=[BarB, Bc], w=[Bk1])
                for c in range(8):
                    cs = slice(c * 64, (c + 1) * 64)
                    T.op("dve", lambda: V.tensor_tensor_scan(out=Gc[:, cs], data0=lnf[:, cs], data1=zeros[:],
                                                             initial=0.0, op0=ALU.add, op1=ALU.add),
                         r=[Blnf, Bc], w=[BGc])

            def h2b(h):
                T.op("act", lambda: A.activation(out=eG, in_=Gc, func=AF.Exp), r=[BGc], w=[BeG])
                T.op("act", lambda: A.activation(out=erG, in_=Gc, func=AF.Exp, scale=-1.0), r=[BGc], w=[BerG])
                T.op("dve", lambda: V.scalar_tensor_tensor(out=QaT[:, h, :], in0=Qs2[:, h, :], scalar=0.5, in1=eG,
                                                           op0=ALU.mult, op1=ALU.mult), r=[ByhT, BeG], w=[BQaT])
                T.op("dve", lambda: V.tensor_copy(Plast[:, h, :], eG.rearrange("p (c t) -> p c t", t=64)[:, :, 63]),
                     r=[BeG], w=[BPl])
                T.op("pool", lambda: G.tensor_tensor(out=kk, in0=k1, in1=erG, op=ALU.mult), r=[Bk1, BerG], w=[Bkk])
                T.op("pool", lambda: G.tensor_copy(KaT[:, h, :], kk), r=[Bkk], w=[BarD])
                T.op("dve", lambda: V.tensor_tensor(
                    out=KdT[:, h, :].rearrange("p (c t) -> p c t", t=64), in0=kk.rearrange("p (c t) -> p c t", t=64),
                    in1=Plast[:, h, :].unsqueeze(2).broadcast_to([128, 8, 64]), op=ALU.mult),
                     r=[Bkk, BPl], w=[BarD])

            def hq_unit(h):
                def hq_evac(pbk, bpbk):
                    th, bth = tanh_evac(pbk, bpbk)
                    T.op("dve", lambda: V.scalar_tensor_tensor(out=Qs2[:, h, :], in0=th, scalar=1.0, in1=pbk[:],
                                                               op0=ALU.add, op1=ALU.mult), r=[bth, bpbk], w=[ByhT])
                proj_fm(f"HQ{h}", hq_evac)
                unit_done()

            def hf_unit(h):
                proj_fm(f"HF{h}", lambda pbk, bpbk: T.op(
                    "act", lambda: A.activation(out=thf[:, h, :], in_=pbk[:], func=AF.Tanh, scale=0.5), r=[bpbk],
                    w=[BarB]))
                unit_done()

            def hg_unit(h):
                def hg_evac(pbk, bpbk):
                    th, bth = tanh_evac(pbk, bpbk)
                    T.op("dve", lambda: V.scalar_tensor_tensor(out=Gs2[:, h, :], in0=th, scalar=1.0, in1=pbk[:],
                                                               op0=ALU.add, op1=ALU.mult), r=[bth, bpbk], w=[BGs2])
                proj_fm(f"HG{h}", hg_evac)
                unit_done()

            hstream = []
            for h in range(8):
                hstream.append(lambda h=h: hq_unit(h))
                if h == 0:
                    hstream.append(lambda h=h: hf_unit(h))
                else:
                    hstream.append(lambda h=h: (hf_unit(h), h2b_guard(h - 1)))
                hstream.append(lambda h=h: (hg_unit(h), h2a_guard(h)))
            hpos = [0]
            h2state = {"a": False, "b": False, "att_done": False}

            def h2a_guard(h):
                if not h2state["a"]:
                    enter(BH2)
                    h2state["a"] = True
                h2a(h)

            def h2b_guard(h):
                assert h2state["att_done"]
                if not h2state["b"]:
                    enter([BarD])
                    h2state["b"] = True
                h2b(h)

            def next_h():
                f_ = hstream[hpos[0]]
                hpos[0] += 1
                f_()

            q_unit(0, 0)
            q_unit(0, 1)
            ag_unit(0, 0)
            ag_unit(0, 1)
            for g in range(4):
                if g < 3:
                    fill = [lambda g=g: q_unit(g + 1, 0), lambda g=g: q_unit(g + 1, 1),
                            lambda g=g: ag_unit(g + 1, 0), lambda g=g: ag_unit(g + 1, 1)]
                else:
                    fill = [next_h, next_h, next_h, next_h]
                if g == 0:
                    stage_a(g, 0)
                for n in range(4):
                    if n + 1 < 4:
                        stage_a(g, n + 1)
                    elif g + 1 < 4:
                        stage_a(g + 1, 0)
                    stage_b(g, n)
                    fill[n]()
                    stage_c(g, n)
            T.op("pool", lambda: G.tensor_copy(KT[:, :, 0:128], KT[:, :, 512:640]), r=[BKT], w=[BKT])
            T.op("pool", lambda: G.tensor_copy(Vaug[:, 0, :, 0:64], Vaug[:, 4, :, 0:64]), r=[BVa], w=[BVa])
            h2state["att_done"] = True
            if STOP == "attn":
                return
            while hpos[0] < len(hstream):
                next_h()
            h2b_guard(7)
            if gcount[0] % 4 != 0:
                wdone()

            if STOP == "h2":
                return
            if first:
                T.op("pool", lambda: G.memset(S32[:], 0.0), w=BS32)
                T.op("pool", lambda: G.memset(Sbf[:], 0.0), w=[bb for pr_ in BSb for bb in pr_])
            enter(Bsq + Blr + Bot + Btha)
            tha = arE[:, 5120:6144].rearrange("p (s c) -> p s c", c=TT)
            ecnt = [0]

            def early():
                c = ecnt[0]
                if c >= 16:
                    return
                ecnt[0] += 1
                s2 = c % 2
                wga, bwga = wunit(f"GA{c}")
                wwa, bwwa = wunit(f"WA{c}")
                for k in range(16):
                    mm(pb[4][:], wga[:, k, :], hT[:, k, :], k == 0, k == 15, bwga + [BhT], [Bpb[4]], signal=(k == 15))
                T.op("act", lambda: A.activation(out=tha[:, s2, :], in_=pb[4][:], func=AF.Tanh, scale=0.5),
                     r=[Bpb[4]], w=[Btha[s2]])
                for k in range(8):
                    mm(pb[5][:], wwa[:, k, :], yaT[:, k, :], k == 0, k == 7, bwwa + [ByaT], [Bpb[5]], signal=(k == 7))
                T.op("dve", lambda: V.scalar_tensor_tensor(out=mergedT[:, c, :], in0=tha[:, s2, :], scalar=1.0,
                                                           in1=pb[5][:], op0=ALU.add, op1=ALU.mult),
                     r=[Btha[s2], Bpb[5]], w=[BarB])
                if c % 2 == 1:
                    wdone()

            early()
            early()
            sq = arE[:, 0:1024].rearrange("p (s c) -> p s c", c=128)
            lr = arE32[:, 512:1536].rearrange("p (g c) -> p g c", c=512)
            ot = arE32[:, 1536:2560].rearrange("p (g c) -> p g c", c=512)
            for b in range(4):
                bc = slice(b * 128, (b + 1) * 128)
                for hg in range(2):
                    tb, ab = 7 - hg, 2 + hg
                    for hh in range(4):
                        h = 4 * hg + hh
                        T.op("pe", lambda: PE.transpose(ptbs[tb][:, hh, :], KdT[:, h, bc], ident[:]), r=[BarD, Bc],
                             w=[Bpb[tb]], signal=(hh == 3))
                    for hh in range(4):
                        h = 4 * hg + hh
                        mm(pb[ab][:, hh * 128:(hh + 1) * 128], KaT[:, h, bc], QaT[:, h, bc], True, True,
                           [BarD, BQaT], [Bpb[ab]], signal=(hh == 3))
                    T.op("act", lambda: A.copy(Kdtm[:, 4 * hg:4 * hg + 4, :], ptbs[tb][:, 0:4, :]), r=[Bpb[tb]],
                         w=BKdtm[4 * hg:4 * hg + 4])
                    T.op("dve", lambda: V.tensor_tensor(
                        out=Am[:, 4 * hg:4 * hg + 4, :], in0=pb[ab][:].rearrange("p (h c) -> p h c", c=128),
                        in1=hmask[:].unsqueeze(1).broadcast_to([128, 4, 128]), op=ALU.mult),
                         r=[Bpb[ab], Bc], w=BAm[4 * hg:4 * hg + 4])
                early()
                for hg in range(2):
                    u0, u1 = 0, 6 + hg
                    for hh in range(4):
                        h = 4 * hg + hh
                        hc = slice(h * 128, (h + 1) * 128)
                        mm(pb[u0][:, hh * 128:(hh + 1) * 128], Kdtm[0:64, h, :], Vh[0:64, b, hc], True, True,
                           [BKdtm[h], BVh], [Bpb[u0]], signal=False)
                        mm(pb[u1][:, hh * 128:(hh + 1) * 128], Kdtm[64:128, h, :], Vh[64:128, b, hc], True, True,
                           [BKdtm[h], BVh], [Bpb[u1]], signal=(hh == 3))
                    for hh in range(4):
                        h = 4 * hg + hh
                        T.op("dve", lambda: V.scalar_tensor_tensor(
                            out=S32[:, h, :], in0=S32[:, h, :], scalar=Plast[:, h, 2 * b:2 * b + 1],
                            in1=pb[u0][:, hh * 128:(hh + 1) * 128], op0=ALU.mult, op1=ALU.add),
                             r=[BS32[h], BPl, Bpb[u0]], w=[BS32[h]])
                        T.op("act", lambda: A.copy(Sbf[:, h, 0, :], S32[:, h, :]), r=[BS32[h]], w=[BSb[h][0]])
                early()
                for hg in range(2):
                    u1, ob = 6 + hg, 2 + hg
                    for hh in range(4):
                        h = 4 * hg + hh
                        hc = slice(h * 128, (h + 1) * 128)
                        oc_ = slice(hh * 128, (hh + 1) * 128)
                        mm(pb[ob][:, oc_], Vh[:, b, hc], Am[:, h, :], True, False, [BVh, BAm[h]], [Bpb[ob]],
                           signal=False)
                        mm(pb[ob][:, hh * 128:hh * 128 + 64], Sbf[:, h, 1, :], QaT[:, h, b * 128:b * 128 + 64],
                           False, False, [BSb[h][1], BQaT], [Bpb[ob]], signal=False)
                        mm(pb[ob][:, hh * 128 + 64:(hh + 1) * 128], Sbf[:, h, 0, :],
                           QaT[:, h, b * 128 + 64:b * 128 + 128], False, True, [BSb[h][0], BQaT], [Bpb[ob]])
                    for hh in range(4):
                        h = 4 * hg + hh
                        T.op("dve", lambda: V.scalar_tensor_tensor(
                            out=S32[:, h, :], in0=S32[:, h, :], scalar=Plast[:, h, 2 * b + 1:2 * b + 2],
                            in1=pb[u1][:, hh * 128:(hh + 1) * 128], op0=ALU.mult, op1=ALU.add),
                             r=[BS32[h], BPl, Bpb[u1]], w=[BS32[h]])
                        T.op("act", lambda: A.copy(Sbf[:, h, 1, :], S32[:, h, :]), r=[BS32[h]], w=[BSb[h][1]])
                    T.op("act", lambda: A.activation(out=sq[:, 4 * hg:4 * hg + 4, :],
                                                     in_=pb[ob][:].rearrange("p (h c) -> p h c", c=128),
                                                     func=AF.Square), r=[Bpb[ob]], w=[Bsq[hg]])
                early()
                for hg in range(2):
                    ob, sb_ = 2 + hg, 6 + hg
                    mm(pb[sb_][:], ones[:], sq[:, 4 * hg:4 * hg + 4, :], True, True, [Bc, Bsq[hg]], [Bpb[sb_]])
                    T.op("act", lambda: A.activation(out=lr[:, hg, :], in_=pb[sb_][:], func=AF.Ln,
                                                     scale=1.0 / 128.0, bias=epsb[:]), r=[Bpb[sb_], Bc], w=[Blr[hg]])
                    T.op("act", lambda: A.activation(out=lr[:, hg, :], in_=lr[:, hg, :], func=AF.Exp, scale=-0.5),
                         r=[Blr[hg]], w=[Blr[hg]])
                    for hh in range(4):
                        h = 4 * hg + hh
                        oc_ = slice(hh * 128, (hh + 1) * 128)
                        T.op("dve", lambda: V.scalar_tensor_tensor(out=ot[:, hg, oc_], in0=pb[ob][:, oc_],
                                                                   scalar=coef[:, 3, h:h + 1], in1=lr[:, hg, oc_],
                                                                   op0=ALU.mult, op1=ALU.mult),
                             r=[Bpb[ob], Bc, Blr[hg]], w=[Bot[hg]])
                    T.op("pool", lambda: G.tensor_tensor(
                        out=yhT[:, 4 * hg:4 * hg + 4, bc], in0=ot[:, hg, :].rearrange("p (h c) -> p h c", c=128),
                        in1=Gs2[:, 4 * hg:4 * hg + 4, bc], op=ALU.mult), r=[Bot[hg], BGs2], w=[ByhT])
                early()

            if STOP == "h3":
                return
            while ecnt[0] < 16:
                early()
            enter([BarE])
            thh = arE[:, 0:1024].rearrange("p (s c) -> p s c", c=TT)
            m2 = arE32[:, 1024:2048].rearrange("p (s c) -> p s c", c=TT)
            for c in range(16):
                s2 = c % 2
                wgh, bwgh = wunit(f"GH{c}")
                wwh, bwwh = wunit(f"WH{c}")
                bgh, buh = 0 + s2, 2 + s2
                for k in range(16):
                    mm(pb[bgh][:], wgh[:, k, :], hT[:, k, :], k == 0, k == 15, bwgh + [BhT], [Bpb[bgh]], signal=(k == 15))
                T.op("act", lambda: A.activation(out=thh[:, s2, :], in_=pb[bgh][:], func=AF.Tanh, scale=0.5),
                     r=[Bpb[bgh]], w=[BarE])
                for k in range(8):
                    mm(pb[buh][:], wwh[:, k, :], yhT[:, k, :], k == 0, k == 7, bwwh + [ByhT], [Bpb[buh]], signal=(k == 7))
                T.op("dve", lambda: V.scalar_tensor_tensor(out=m2[:, s2, :], in0=thh[:, s2, :], scalar=1.0,
                                                           in1=pb[buh][:], op0=ALU.add, op1=ALU.mult),
                     r=[BarE, Bpb[buh]], w=[BarE])
                T.op("pool", lambda: G.tensor_tensor(out=mergedT[:, c, :], in0=mergedT[:, c, :], in1=m2[:, s2, :],
                                                     op=ALU.add), r=[BarE], w=[BarB])
                if c % 2 == 1:
                    wdone()

            if STOP == "merge":
                return
            enter([Bysb])
            nxt = t + 1 if t + 1 < NSEQ * NT else None
            if nxt is not None:
                phase_x_pre(nxt, 0)
            xres = [xa[:, 0, :], xa[:, 1, :], arC[:].bitcast(F32), xa[:, 0, :]]
            Bres = [[Bxa[0]], [Bxa[1]], BarC, [Bxa[0]]]
            dres = [dxl[0], dxl[1], dxl2, dxl[0]]
            dsto = [dxs[0], dxs[1], dxs2, dxs[0]]

            def reload(b):
                r0_ = tok0 + 128 * b
                T.dma("sp", xres[b], x[r0_:r0_ + 128, :], dres[b], w=Bres[b])

            oc = [0]
            for i in range(4):
                wv, bw = wmov([f"WO{4 * i + k}" for k in range(4)])
                for b in range(4):
                    bk = oc[0] % 6
                    oc[0] += 1
                    for k in range(16):
                        mm(pb[bk][:].rearrange("p (u c) -> p u c", c=128), mergedT[:, k, b * 128:(b + 1) * 128],
                           wv[:, :, k, :], k == 0, k == 15, bw + [BarB], [Bpb[bk]], signal=(k == 15))
                    T.op("act", lambda: A.activation(out=junk[:], in_=pb[bk][:], func=AF.Square,
                                                     accum_out=ssq[:, b, i:i + 1]), r=[Bpb[bk]], w=[Bjunk, Bssq[b]])
                    T.op("dve", lambda: V.tensor_tensor(out=ysb[:, b, i * 512:(i + 1) * 512], in0=pb[bk][:],
                                                        in1=npost[:, i * 512:(i + 1) * 512], op=ALU.mult),
                         r=[Bpb[bk], Bc], w=[Bysb])
                if nxt is not None:
                    phase_x_tr(i)
                wdone()
                if nxt is not None and i + 1 < 4:
                    phase_x_pre(nxt, i + 1)
                if i == 1:
                    reload(0)
                if i == 2:
                    reload(1)
            reload(2)

            def tail(b):
                r0 = tok0 + 128 * b
                T.op("dve", lambda: V.reduce_sum(out=ss2[:, b:b + 1], in_=ssq[:, b, :], axis=mybir.AxisListType.X),
                     r=[Bssq[b]], w=[Bss2[b]])
                T.op("pool", lambda: G.tensor_scalar(out=rs2[:, b:b + 1], in0=ss2[:, b:b + 1], scalar1=1.0 / D,
                                                     scalar2=4.0 * EPS, op0=ALU.mult, op1=ALU.add),
                     r=[Bss2[b]], w=[Brs2[b]])
                T.op("pool", lambda: G.tensor_tensor(out=rs2[:, b:b + 1], in0=rs2[:, b:b + 1], in1=mhalf[:],
                                                     op=ALU.pow), r=[Brs2[b], Bc], w=[Brs2[b]])
                T.op("dve", lambda: V.scalar_tensor_tensor(out=ysb[:, b, :], in0=ysb[:, b, :], scalar=rs2[:, b:b + 1],
                                                           in1=xres[b], op0=ALU.mult, op1=ALU.add),
                     r=[Brs2[b]] + Bres[b], w=[Bysb])
                T.dma("sp", out[r0:r0 + 128, :], ysb[:, b, :], dsto[b], r=[Bysb])

            for b in range(3):
                tail(b)
            reload(3)
            deferred.append(lambda: tail(3))

        epsb = sb("epsb", [128, 1], F32)
        T.op("pool", lambda: G.memset(epsb[:], EPS), w=[Bc])

        for t in range(NSEQ * NT):
            do_tile(t)
        while deferred:
            deferred.pop(0)()
        T.barrier(("sp",))
    return nc


def _t5_bucket(dist):
    max_exact = 16
    d = np.maximum(dist, 0)
    df = np.maximum(d, 1).astype(np.float32)
    large = max_exact + (np.log(df / max_exact) / np.log(128 / max_exact) * (32 - max_exact)).astype(np.int32)
    large = np.minimum(large, 31)
    return np.where(d < max_exact, d, large)


def host_consts(norm_pre, norm_post, rel_bias, attn_sinks, lb_logits, hgrn_norm):
    f32 = np.float32
    s = np.arange(128)[:, None]
    q = np.arange(128)[None, :]
    biasg = np.zeros((128, 4, 2, 2, 2, 128), f32)
    amask = np.zeros((128, 4, 2, 2, 2, 128), f32)
    for blk in range(2):
        dist = q + 128 - (s + 128 * blk)
        valid = (dist >= 0) & (dist < 128)
        bk = _t5_bucket(dist)
        for g in range(4):
            for p in range(2):
                for j in range(2):
                    head = 4 * g + 2 * j + p
                    biasg[:, g, p, blk, j, :] = rel_bias[bk, head]
                    amask[:, g, p, blk, j, :] = valid
    hm = ((q >= s) & ((q // 64) == (s // 64))).astype(f32)
    return {
        "npre": np.ascontiguousarray(norm_pre.reshape(16, 128).T).astype(f32),
        "npost": np.ascontiguousarray(np.broadcast_to(norm_post.reshape(1, D), (128, D))).astype(f32),
        "biasg": biasg.reshape(128, 4096),
        "amask": amask.reshape(128, 4096),
        "sinks": np.ascontiguousarray(np.broadcast_to(attn_sinks.reshape(1, 16), (128, 16))).astype(f32),
        "lbl": np.ascontiguousarray(lb_logits.reshape(2, 8, 128).transpose(2, 0, 1).reshape(128, 16)).astype(f32),
        "hgain": np.ascontiguousarray(hgrn_norm.reshape(8, 128).T).astype(f32),
        "ident": np.eye(128, dtype=f32).astype(ml_dtypes.bfloat16),
        "hmask": hm.astype(ml_dtypes.bfloat16),
    }


_NC_CACHE = {}


def run(x, norm_pre, w_in, rel_bias, attn_sinks, lb_logits, hgrn_norm, w_branch_attn, w_branch_hgrn, w_out,
        norm_post, n_cores):
    x = np.asarray(x, np.float32)
    B, S, _ = x.shape
    assert B % n_cores == 0
    nseq = B // n_cores
    key = (nseq, S)
    if key not in _NC_CACHE:
        _NC_CACHE[key] = build_nc(nseq, S)
    nc = _NC_CACHE[key]
    consts = host_consts(np.asarray(norm_pre, np.float32)[0], np.asarray(norm_post, np.float32)[0],
                         np.asarray(rel_bias, np.float32), np.asarray(attn_sinks, np.float32)[0],
                         np.asarray(lb_logits, np.float32), np.asarray(hgrn_norm, np.float32)[0])
    shared = dict(consts)
    shared["w_in"] = np.ascontiguousarray(np.asarray(w_in, np.float32)[0])
    shared["w_a"] = np.ascontiguousarray(np.asarray(w_branch_attn, np.float32)[0])
    shared["w_h"] = np.ascontiguousarray(np.asarray(w_branch_hgrn, np.float32)[0])
    shared["w_o"] = np.ascontiguousarray(np.asarray(w_out, np.float32)[0])
    in_maps = []
    for c in range(n_cores):
        m = dict(shared)
        m["x"] = np.ascontiguousarray(x[c * nseq:(c + 1) * nseq].reshape(nseq * S, D))
        in_maps.append(m)
    res = run_bass_kernel_spmd(nc, in_maps, core_ids=list(range(n_cores)))
    outs = [np.asarray(r["out"], np.float32).reshape(nseq, S, D) for r in res.results]
    return np.concatenate(outs, axis=0)


def kernel(x, norm_pre, w_in, rel_bias, attn_sinks, lb_logits, hgrn_norm, w_branch_attn, w_branch_hgrn, w_out,
           norm_post):
    return run(x, norm_pre, w_in, rel_bias, attn_sinks, lb_logits, hgrn_norm, w_branch_attn, w_branch_hgrn, w_out,
               norm_post, n_cores=8)
```

```python
import contextlib
import numpy as np
import ml_dtypes
import concourse.bass as bass
import concourse.mybir as mybir
from concourse.bass_utils import run_bass_kernel_spmd

F32 = mybir.dt.float32
BF16 = mybir.dt.bfloat16
AF = mybir.ActivationFunctionType
ALU = mybir.AluOpType

D = 2048
NKC = 16
TT = 512
EPS = 1e-6
AQ0, AK0, AV0, AG0, HQ0, HF0, HI0, HG0, GA0, GH0 = 0, 1024, 1280, 1536, 2560, 3584, 4608, 5632, 6656, 8704
IN_W = 10752
NW = 2
STOP = None
FUSE0 = True
FUSEALL = False
FUSE_PAIRS = True


class Tok:
    __slots__ = ("sem", "val")

    def __init__(self, sem, val):
        self.sem = sem
        self.val = val


class Buf:
    __slots__ = ("name", "w", "r", "excl")

    def __init__(self, name, excl=False):
        self.name = name
        self.w = None
        self.r = {}
        self.excl = excl


class DSem:
    def __init__(self, sem):
        self.sem = sem
        self.cnt = 0
        self.open = []


class Trk:
    def __init__(self, nc, es):
        self.nc = nc
        self.es = es
        self.eng = {"pe": nc.tensor, "act": nc.scalar, "dve": nc.vector, "pool": nc.gpsimd, "sp": nc.sync}
        self.sem = {e: es.enter_context(nc.semaphore("s_" + e)) for e in ("pe", "act", "dve", "pool")}
        self.cnt = {e: 0 for e in self.sem}
        self.waited = {e: {} for e in self.eng}
        self.pending_pe = []
        self.dsems = []

    def dsem(self, name):
        d = DSem(self.es.enter_context(self.nc.semaphore(name)))
        self.dsems.append(d)
        return d

    def _wait(self, e, r, w):
        need = {}

        def add(t):
            if t is None:
                return
            if e == "pe" and t.sem is self.sem["pe"]:
                return
            assert t.val is not None, "pending PE token used as dependency"
            k = id(t.sem)
            if k not in need or need[k][1] < t.val:
                need[k] = (t.sem, t.val)

        for b in r:
            add(b.w)
        for b in w:
            add(b.w)
            for t in b.r.values():
                add(t)
        for k, (sem, val) in need.items():
            if self.waited[e].get(k, 0) >= val:
                continue
            self.eng[e].wait_ge(sem, val)
            self.waited[e][k] = val

    def _record(self, r, w, tok):
        for b in r:
            b.r[id(tok.sem)] = tok
        for b in w:
            b.w = tok
            b.r = {}

    def op(self, e, fn, r=(), w=(), signal=True):
        w = list(w) + [b for b in r if b.excl]
        r = [b for b in r if not b.excl]
        self._wait(e, r, w)
        inst = fn()
        if e == "pe" and not signal:
            tok = Tok(self.sem["pe"], None)
            self.pending_pe.append(tok)
        else:
            self.cnt[e] += 1
            inst.then_inc(self.sem[e], 1)
            tok = Tok(self.sem[e], self.cnt[e])
            if e == "pe":
                for p in self.pending_pe:
                    p.val = self.cnt[e]
                self.pending_pe = []
        self._record(r, w, tok)
        return inst

    def dma(self, q, out, in_, dsem, r=(), w=()):
        self._wait(q, r, w)
        inst = self.eng[q].dma_start(out=out, in_=in_)
        dsem.cnt += 16
        inst.then_inc(dsem.sem, 16)
        tok = Tok(dsem.sem, dsem.cnt)
        dsem.open.append(tok)
        self._record(r, w, tok)
        return inst

    def close_batch(self, dsem):
        for t in dsem.open:
            t.val = dsem.cnt
        dsem.open = []

    def inherit(self, dst, src):
        for d in dst:
            for b in src:
                toks = list(b.r.values()) + ([b.w] if b.w is not None else [])
                for t in toks:
                    k = id(t.sem)
                    if t.val is None:
                        d.r[("p", id(t))] = t
                    elif k not in d.r or d.r[k].val is None or d.r[k].val < t.val:
                        d.r[k] = t

    def barrier(self, engines=("pe", "act", "dve", "pool", "sp")):
        assert not self.pending_pe
        for e in engines:
            for s in ("pe", "act", "dve", "pool"):
                if s == e and e == "pe":
                    continue
                v = self.cnt[s]
                k = id(self.sem[s])
                if v > 0 and self.waited[e].get(k, 0) < v:
                    self.eng[e].wait_ge(self.sem[s], v)
                    self.waited[e][k] = v
            for d in self.dsems:
                k = id(d.sem)
                if d.cnt > 0 and self.waited[e].get(k, 0) < d.cnt:
                    self.eng[e].wait_ge(d.sem, d.cnt)
                    self.waited[e][k] = d.cnt


def make_plan():
    units = {}

    def U(name, src, kc, col0, dup=False):
        units[name] = dict(src=src, kc=kc, col0=col0, dup=dup)

    for i in range(2):
        U(f"AV{i}", "win", 16, AV0 + 128 * i)
    for g in range(4):
        U(f"KD{g}", "win", 16, AK0 + 64 * g, dup=True)
        for j in range(2):
            U(f"Q{g}{j}", "win", 16, AQ0 + 64 * (4 * g + 2 * j))
            U(f"AG{g}{j}", "win", 16, AG0 + 64 * (4 * g + 2 * j))
    for h in range(8):
        U(f"HI{h}", "win", 16, HI0 + 128 * h)
        U(f"HQ{h}", "win", 16, HQ0 + 128 * h)
        U(f"HF{h}", "win", 16, HF0 + 128 * h)
        U(f"HG{h}", "win", 16, HG0 + 128 * h)
    for c in range(16):
        U(f"GA{c}", "win", 16, GA0 + 128 * c)
        U(f"GH{c}", "win", 16, GH0 + 128 * c)
        U(f"WA{c}", "wa", 8, 128 * c)
        U(f"WH{c}", "wh", 8, 128 * c)
        U(f"WO{c}", "wo", 16, 128 * c)

    groups = [["AV0", "AV1", "KD0", "KD1"], [f"HI{h}" for h in range(4)], [f"HI{h}" for h in range(4, 8)]]
    groups.append(["KD2", "KD3"])
    flat = []
    for g in range(4):
        flat += [f"Q{g}0", f"Q{g}1", f"AG{g}0", f"AG{g}1"]
    for h in range(8):
        flat += [f"HQ{h}", f"HF{h}", f"HG{h}"]
    for i in range(0, len(flat), 4):
        groups.append(flat[i:i + 4])
    for j in range(8):
        groups.append([f"GA{2 * j}", f"WA{2 * j}", f"GA{2 * j + 1}", f"WA{2 * j + 1}"])
    for j in range(8):
        groups.append([f"GH{2 * j}", f"WH{2 * j}", f"GH{2 * j + 1}", f"WH{2 * j + 1}"])
    for i in range(4):
        groups.append([f"WO{4 * i + k}" for k in range(4)])
    gsize = []
    for gi, g in enumerate(groups):
        off = 0
        for n in g:
            units[n]["g"] = gi
            units[n]["off"] = off
            off += units[n]["kc"]
        assert off <= 64
        gsize.append(off)

    jobs = []
    jobs.append(("win", 16, AV0, 256, ["AV0", "AV1"]))
    jobs.append(("win", 16, AK0, 256, ["KD0", "KD1", "KD2", "KD3"]))
    for i in range(2):
        jobs.append(("win", 16, HI0 + 512 * i, 512, [f"HI{4 * i + k}" for k in range(4)]))
    for g in range(4):
        jobs.append(("win", 16, AQ0 + 256 * g, 256, [f"Q{g}0", f"Q{g}1"]))
        jobs.append(("win", 16, AG0 + 256 * g, 256, [f"AG{g}0", f"AG{g}1"]))
    for nm, c0 in (("HQ", HQ0), ("HF", HF0), ("HG", HG0)):
        for i in range(2):
            jobs.append(("win", 16, c0 + 512 * i, 512, [f"{nm}{4 * i + k}" for k in range(4)]))
    for nm, src, kc, c0 in (("GA", "win", 16, GA0), ("GH", "win", 16, GH0), ("WA", "wa", 8, 0), ("WH", "wh", 8, 0),
                            ("WO", "wo", 16, 0)):
        for i in range(4):
            jobs.append((src, kc, c0 + 512 * i, 512, [f"{nm}{4 * i + k}" for k in range(4)]))
    jobs.sort(key=lambda j: min(units[n]["g"] for n in j[4]))
    return units, groups, gsize, jobs


def build_nc(NSEQ, S):
    assert S % TT == 0
    NT = S // TT
    NTOK = NSEQ * S
    units, groups, gsize, jobs = make_plan()
    NG = len(groups)

    nc = bass.Bass("TRN2", target_bir_lowering=False)

    def din(name, shape, dt=F32):
        return nc.dram_tensor(name, shape, dt, kind="ExternalInput").ap()

    x = din("x", [NTOK, D])
    wsrc = {"win": din("w_in", [D, IN_W]), "wa": din("w_a", [1024, D]), "wh": din("w_h", [1024, D]),
            "wo": din("w_o", [D, D])}
    d_npre = din("npre", [128, 16])
    d_npost = din("npost", [128, D])
    d_biasg = din("biasg", [128, 4096])
    d_amask = din("amask", [128, 4096])
    d_sinks = din("sinks", [128, 16])
    d_lbl = din("lbl", [128, 16])
    d_hgain = din("hgain", [128, 8])
    d_ident = din("ident", [128, 128], BF16)
    d_hmask = din("hmask", [128, 128], BF16)
    out = nc.dram_tensor("out", [NTOK, D], F32, kind="ExternalOutput").ap()
    wsc = nc.dram_tensor("wsc", [NG, 128, 8192], BF16, kind="Internal").ap()

    with contextlib.ExitStack() as es:
        T = Trk(nc, es)
        V, A, G, PE = nc.vector, nc.scalar, nc.gpsimd, nc.tensor

        def sb(name, shape, dt, stack=es):
            return stack.enter_context(nc.sbuf_tensor("sb_" + name, shape, dt))

        def psum(name, shape, dt):
            return es.enter_context(nc.psum_tensor("ps_" + name, shape, dt))

        ident = sb("ident", [128, 128], BF16)
        hmask = sb("hmask", [128, 128], BF16)
        ones = sb("ones", [128, 128], BF16)
        zeros = sb("zeros", [128, 64], F32)
        mhalf = sb("mhalf", [128, 1], F32)
        npre = sb("npre", [128, 16], F32)
        npost = sb("npost", [128, D], F32)
        Emb = sb("Emb", [128, 4, 1024], BF16)
        es2 = sb("es2", [128, 16], F32)
        lbl = sb("lblt", [128, 16], F32)
        coef = sb("coef", [128, 4, 8], F32)
        hgain = sb("hgaint", [128, 8], F32)
        Bc = Buf("consts")
        dC = T.dsem("dC")

        T.dma("sp", ident[:], d_ident, dC, w=[Bc])
        T.dma("sp", hmask[:], d_hmask, dC, w=[Bc])
        T.dma("sp", npre[:], d_npre, dC, w=[Bc])
        T.dma("sp", npost[:], d_npost, dC, w=[Bc])
        T.dma("sp", es2[:], d_sinks, dC, w=[Bc])
        T.dma("sp", lbl[:], d_lbl, dC, w=[Bc])
        T.dma("sp", hgain[:], d_hgain, dC, w=[Bc])
        T.close_batch(dC)
        T.op("pool", lambda: G.memset(ones[:], 1.0), w=[Bc])
        T.op("pool", lambda: G.memset(zeros[:], 0.0), w=[Bc])
        T.op("pool", lambda: G.memset(mhalf[:], -0.5), w=[Bc])

        with contextlib.ExitStack() as ps_:
            NST = 3 if not FUSE0 else 2
            st32 = [sb(f"st32_{i}", [128, 16 * 512], F32, ps_) for i in range(NST)]
            st16 = [sb(f"st16_{i}", [128, 4, 2048], BF16, ps_) for i in range(NST if not FUSE0 else 0)]
            etmp = st32[0][:, 0:4096]
            mtmp = st32[1][:, 0:4096]
            B32 = [Buf(f"st32_{i}") for i in range(NST)]
            B16 = [Buf(f"st16_{i}") for i in range(NST)]
            d32 = [T.dsem(f"d32_{i}") for i in range(NST)]
            d16 = [T.dsem(f"d16_{i}") for i in range(NST)]
            Bet = Buf("etmp")

            dC2 = T.dsem("dC2")
            T.dma("sp", etmp, d_biasg, dC2, w=[Bet, B32[0]])
            T.dma("sp", mtmp, d_amask, dC2, w=[Bet, B32[1]])
            T.close_batch(dC2)
            T.op("act", lambda: A.activation(out=etmp, in_=etmp, func=AF.Exp), r=[Bet], w=[Bet, B32[0]])
            T.op("dve", lambda: V.tensor_tensor(out=Emb[:].rearrange("p g c -> p (g c)"), in0=etmp, in1=mtmp,
                                                op=ALU.mult), r=[Bet, B32[0], B32[1]], w=[Bc])
            T.op("act", lambda: A.activation(out=es2[:], in_=es2[:], func=AF.Exp), r=[Bc], w=[Bc])
            T.op("dve", lambda: V.tensor_scalar(out=es2[:], in0=es2[:], scalar1=2.0, scalar2=None, op0=ALU.mult),
                 r=[Bc], w=[Bc])
            T.op("dve", lambda: V.tensor_tensor(out=lbl[:, 0:8], in0=lbl[:, 0:8], in1=lbl[:, 8:16], op=ALU.subtract),
                 r=[Bc], w=[Bc])
            T.op("act", lambda: A.activation(out=lbl[:, 0:8], in_=lbl[:, 0:8], func=AF.Tanh, scale=0.5), r=[Bc], w=[Bc])
            T.op("dve", lambda: V.tensor_scalar(out=coef[:, 0, :], in0=lbl[:, 0:8], scalar1=0.25, scalar2=0.75,
                                                op0=ALU.mult, op1=ALU.add), r=[Bc], w=[Bc])
            T.op("dve", lambda: V.tensor_scalar(out=coef[:, 1, :], in0=lbl[:, 0:8], scalar1=-0.25, scalar2=0.25,
                                                op0=ALU.mult, op1=ALU.add), r=[Bc], w=[Bc])
            T.op("dve", lambda: V.tensor_scalar(out=coef[:, 2, :], in0=lbl[:, 0:8], scalar1=0.25, scalar2=-0.25,
                                                op0=ALU.mult, op1=ALU.add), r=[Bc], w=[Bc])
            T.op("dve", lambda: V.tensor_scalar(out=coef[:, 3, :], in0=hgain[:], scalar1=0.5, scalar2=None,
                                                op0=ALU.mult), r=[Bc], w=[Bc])

            if not FUSE0:
                def job_load(ji):
                    if ji >= len(jobs):
                        return
                    src, kc, col0, ncols, names = jobs[ji]
                    s_ = ji % NST
                    sv = st32[s_][:, 0:kc * ncols].rearrange("p (k c) -> p k c", c=ncols)
                    T.dma("sp", sv, wsrc[src].rearrange("(k p) c -> p k c", p=128)[:, :, col0:col0 + ncols], d32[s_],
                          w=[B32[s_]])

                for ji in range(NST):
                    job_load(ji)
                for ji, (src, kc, col0, ncols, names) in enumerate(jobs):
                    s = ji % NST
                    sv = st32[s][:, 0:kc * ncols].rearrange("p (k c) -> p k c", c=ncols)
                    for ui, n in enumerate(names):
                        u = units[n]
                        ov = st16[s][:, ui, 0:kc * 128].rearrange("p (k c) -> p k c", c=128)
                        eng = "act" if (ji + ui) % 2 == 0 else "dve"
                        cp = (lambda o, i: A.copy(o, i)) if eng == "act" else (lambda o, i: V.tensor_copy(o, i))
                        if u["dup"]:
                            iv = sv[:, :, 64 * ui:64 * ui + 64]
                            T.op(eng, lambda: cp(ov[:, :, 0:64], iv), r=[B32[s]], w=[B16[s]])
                            T.op(eng, lambda: cp(ov[:, :, 64:128], iv), r=[B32[s]], w=[B16[s]])
                        else:
                            iv = sv[:, :, 128 * ui:128 * ui + 128]
                            T.op(eng, lambda: cp(ov, iv), r=[B32[s]], w=[B16[s]])
                    for ui, n in enumerate(names):
                        u = units[n]
                        T.dma("sp", wsc[u["g"], :, u["off"] * 128:(u["off"] + kc) * 128], st16[s][:, ui, 0:kc * 128],
                              d16[s], r=[B16[s]])
                    job_load(ji + NST)
            T.barrier()

        arA = sb("arA", [128, 16384], BF16)
        hT = arA[:, 0:8192].rearrange("p (k t) -> p k t", t=TT)
        yaT = arA[:, 8192:12288].rearrange("p (k t) -> p k t", t=TT)
        yhT = arA[:, 12288:16384].rearrange("p (k t) -> p k t", t=TT)
        Qs2 = yhT
        BhT, ByaT, ByhT = Buf("hT"), Buf("yaT"), Buf("yhT")

        arB = sb("arB", [128, 8192], BF16)
        thf = arB[:].bitcast(F32).rearrange("p (h t) -> p h t", t=TT)
        mergedT = arB[:].rearrange("p (k t) -> p k t", t=TT)
        BarB = Buf("arB")

        arC = sb("arC", [128, 4096], BF16)
        hb = arC[:].rearrange("p (s c) -> p s c", c=D)
        tmpo = arC[:].bitcast(F32)
        BarC = [Buf("arC0"), Buf("arC1")]

        arDE = sb("arDE", [128, 16384], BF16)
        arD = arDE[:, 0:9216]
        ysb = arDE[:].bitcast(F32).rearrange("p (b c) -> p b c", c=D)
        Bysb = Buf("ysb")
        QT = arD[:, 0:2048].rearrange("p (s j t) -> p s j t", s=2, j=2)
        AGs = arD[:, 2048:4096].rearrange("p (s j t) -> p s j t", s=2, j=2)
        Pe = arD[:, 4096:6144].rearrange("p (s c) -> p s c", s=2)
        PT = arD[:, 6144:8192].rearrange("p (s c) -> p s c", s=2)
        yatm = arD[:, 8192:8704].rearrange("p (s c) -> p s c", s=2)
        KaT = arD[:, 0:4096].rearrange("p (h t) -> p h t", t=TT)
        KdT = arD[:, 4096:8192].rearrange("p (h t) -> p h t", t=TT)
        BarD = Buf("arD")
        BQT = [Buf("QT0"), Buf("QT1")]
        BAGs = [Buf("AGs0"), Buf("AGs1")]
        BPe = [Buf("Pe0"), Buf("Pe1")]
        BPT = [Buf("PT0"), Buf("PT1")]
        BPT2 = [Buf("PT0b"), Buf("PT1b")]
        Byatm = [Buf("yatm0"), Buf("yatm1")]
        Batt = BQT + BAGs + BPe + BPT + BPT2 + Byatm

        arE = arDE[:, 9216:15360]
        arE32 = arE.bitcast(F32)
        BarE = Buf("arE")
        BH2 = [Buf(f"h2_{n}") for n in ("lnf", "Gc", "eG", "erG", "k1", "kk")]
        Bsq = [Buf(f"sq{i}") for i in range(4)]
        Blr = [Buf(f"lr{i}") for i in range(4)]
        Bot = [Buf(f"ot{i}") for i in range(4)]
        Btha = [Buf("tha0"), Buf("tha1")]
        grpDE = [BarD, BarE, Bysb] + Batt + BH2 + Bsq + Blr + Bot

        def enter(bufs):
            T.inherit(bufs, [b_ for b_ in grpDE if all(b_ is not q_ for q_ in bufs)])

        ss2 = sb("ss2", [128, 4], F32)
        rs2 = sb("rs2", [128, 4], F32)
        Bss2 = [Buf(f"ss2{b}") for b in range(4)]
        Brs2 = [Buf(f"rs2{b}") for b in range(4)]

        Wb = [sb(f"Wb{i}", [128, 8192], BF16) for i in range(NW)]
        BW = [[Buf(f"W{i}_{u}") for u in range(8)] for i in range(NW)]

        def wregs(s_, off, kc):
            return BW[s_][off // 8:(off + kc) // 8]
        dW = [T.dsem(f"dW{i}") for i in range(NW)]
        xa = sb("xa", [128, 2, D], F32)
        Bxa = [Buf("xa0"), Buf("xa1")]
        dxl = [T.dsem("dxl0"), T.dsem("dxl1")]
        dxs = [T.dsem("dxs0"), T.dsem("dxs1")]
        dxl2 = T.dsem("dxl2")
        dxs2 = T.dsem("dxs2")
        KT = sb("KT", [128, 4, 640], BF16)
        BKT = Buf("KT")
        Vaug = sb("Vaug", [128, 5, 4, 66], BF16)
        BVa = Buf("Vaug")
        Vh = sb("Vh", [128, 4, 1024], BF16)
        BVh = Buf("Vh")
        Gs2 = sb("Gs2", [128, 8, TT], BF16)
        BGs2 = Buf("Gs2")
        QaT = sb("QaT", [128, 8, TT], BF16)
        BQaT = Buf("QaT")
        Plast = sb("Plast", [128, 8, 8], F32)
        BPl = Buf("Plast")
        S32 = sb("S32", [128, 8, 128], F32)
        Sbf = sb("Sbf", [128, 8, 2, 128], BF16)
        BS32 = [Buf(f"S32_{h}") for h in range(8)]
        BSb = [[Buf(f"Sb{h}_{i}") for i in range(2)] for h in range(8)]
        Kdtm = sb("Kdtm", [128, 8, 128], BF16)
        BKdtm = [Buf(f"Kdtm{h}") for h in range(8)]
        Am = sb("Am", [128, 8, 128], BF16)
        BAm = [Buf(f"Am{h}") for h in range(8)]
        tht = sb("tht", [128, 2, TT], F32)
        Btht = [Buf("tht0"), Buf("tht1")]
        ss = sb("ss", [128, 4], F32)
        rs = sb("rs", [128, 4], F32)
        Bss = [Buf(f"ss{b}") for b in range(4)]
        Brs = [Buf(f"rs{b}") for b in range(4)]
        ssq = sb("ssq", [128, 4, 4], F32)
        Bssq = [Buf(f"ssq{b}") for b in range(4)]
        dd = sb("dd", [128, 2, 4], F32)
        rd = sb("rd", [128, 2, 4], F32)
        Bdd = [Buf("dd0"), Buf("dd1")]
        tha_t = sb("tha", [128, 2, TT], BF16)
        junk = sb("junk", [128, TT], BF16)
        Bjunk = Buf("junk")

        pb = [psum(f"pb{i}", [128, 512], F32) for i in range(8)]
        Bpb = [Buf(f"pb{i}", excl=True) for i in range(8)]
        ptbs = {i: pb[i][:].bitcast(BF16).rearrange("p (s c) -> p s c", c=128) for i in (6, 7)}
        ptb = ptbs[7]

        T.op("pool", lambda: G.memset(Vaug[:], 2.0), w=[BVa])

        gseq = [(t, g) for t in range(NSEQ * NT) for g in range(NG)]
        wstate = {"loaded": 0, "cur": 0}

        Bwsc = [Buf(f"wsc{g}") for g in range(NG)]
        dWs = [T.dsem(f"dWs{i}") for i in range(NW)]
        stg = [xa[:, 0, :], xa[:, 1, :], arC[:].bitcast(F32)]
        Bstg = [[Bxa[0]], [Bxa[1]], BarC]
        dstg = [dxl[0], dxl[1], dxl2]
        cvt = [0]

        PAIRS = FUSE_PAIRS

        def convert_group(g, s):
            names = groups[g]
            jobs_ = []
            i_ = 0
            while i_ < len(names):
                n0 = names[i_]
                u0 = units[n0]
                nxt_ = names[i_ + 1] if i_ + 1 < len(names) else None
                adj = False
                if PAIRS and nxt_ is not None:
                    u1 = units[nxt_]
                    w0 = 64 if u0["dup"] else 128
                    adj = (u1["src"] == u0["src"] and u1["kc"] == u0["kc"] and u1["dup"] == u0["dup"]
                           and u1["col0"] == u0["col0"] + w0)
                if adj:
                    jobs_.append(([n0, nxt_], "pair"))
                    i_ += 2
                else:
                    jobs_.append(([n0], "single"))
                    i_ += 1
            plan = []
            for ns, kind in jobs_:
                u = units[ns[0]]
                w_ = 64 if u["dup"] else 128
                nbytes = u["kc"] * w_ * len(ns) * 4
                plan.append((ns, "P" if nbytes > 8192 else None))
            free_small = [0, 1, 2]

            state = {"used_P": False}
            loads = []

            def do_load(ns, kind):
                u = units[ns[0]]
                kc = u["kc"]
                w_ = 64 if u["dup"] else 128
                wt = w_ * len(ns)
                src = wsrc[u["src"]].rearrange("(k p) c -> p k c", p=128)[:, :, u["col0"]:u["col0"] + wt]
                if kind == "P":
                    sv = xa[:].rearrange("p s c -> p (s c)")[:, 0:kc * wt].rearrange("p (k c) -> p k c", c=wt)
                    bufs, dsem_ = [Bxa[0], Bxa[1]], dxl[0]
                else:
                    k_ = cvt[0] % 3
                    cvt[0] += 1
                    sv = stg[k_][:, 0:kc * wt].rearrange("p (k c) -> p k c", c=wt)
                    bufs, dsem_ = Bstg[k_], dstg[k_]
                T.dma("sp", sv, src, dsem_, w=bufs)
                return sv, bufs, w_

            ccount = [0]

            def do_casts(ns, sv, bufs, w_):
                for ui, n_ in enumerate(ns):
                    u = units[n_]
                    kc = u["kc"]
                    ov = Wb[s][:, u["off"] * 128:(u["off"] + kc) * 128].rearrange("p (k c) -> p k c", c=128)
                    iv = sv[:, :, ui * w_:(ui + 1) * w_]
                    eng = ("act", "dve")[ccount[0] % 2]
                    ccount[0] += 1
                    cp = (lambda o, i__: A.copy(o, i__)) if eng == "act" else (lambda o, i__: V.tensor_copy(o, i__))
                    bwu = wregs(s, u["off"], kc)
                    if u["dup"]:
                        T.op(eng, lambda: cp(ov[:, :, 0:64], iv), r=bufs, w=bwu)
                        T.op(eng, lambda: cp(ov[:, :, 64:128], iv), r=bufs, w=bwu)
                    else:
                        T.op(eng, lambda: cp(ov, iv), r=bufs, w=bwu)

            pending = list(plan)
            inflight = []
            while pending or inflight:
                launched = True
                while pending and launched:
                    launched = False
                    ns, kind = pending[0]
                    has_p = any(k == "P" for _, k, _ in inflight)
                    n_small = sum(1 for _, k, _ in inflight if k is None)
                    if kind == "P" and not has_p and n_small == 0:
                        inflight.append((ns, "P", do_load(ns, "P")))
                        pending.pop(0)
                        launched = True
                    elif kind is None and ((has_p and n_small == 0) or (not has_p and n_small < 3)):
                        if has_p:
                            cvt[0] = 2
                        inflight.append((ns, None, do_load(ns, None)))
                        pending.pop(0)
                        launched = True
                ns, kind, (sv, bufs, w_) = inflight.pop(0)
                do_casts(ns, sv, bufs, w_)
            n = gsize[g] * 128
            T.dma("sp", wsc[g, :, 0:n], Wb[s][:, 0:n], dWs[s], r=BW[s], w=[Bwsc[g]])

        def emit_load(i):
            if i >= len(gseq):
                return
            t_, g = gseq[i]
            s = i % NW
            n = gsize[g] * 128
            if FUSE0 and (t_ == 0 or FUSEALL):
                convert_group(g, s)
            else:
                T.dma("sp", Wb[s][:, 0:n], wsc[g, :, 0:n], dW[s], r=[Bwsc[g]], w=BW[s])
            wstate["loaded"] = i + 1

        def wslot(name):
            g = units[name]["g"]
            i = wstate["cur"]
            assert gseq[i][1] == g, (name, g, gseq[i])
            return i % NW

        def wunit(name):
            u = units[name]
            s = wslot(name)
            v = Wb[s][:, u["off"] * 128:(u["off"] + u["kc"]) * 128].rearrange("p (k c) -> p k c", c=128)
            return v, wregs(s, u["off"], u["kc"])

        def wmov(names):
            u0 = units[names[0]]
            s = wslot(names[0])
            nu = len(names)
            v = Wb[s][:, u0["off"] * 128:(u0["off"] + nu * 16) * 128].rearrange("p (u k c) -> p u k c", u=nu, k=16)
            return v, wregs(s, u0["off"], nu * 16)

        def wdone():
            i = wstate["cur"]
            wstate["cur"] = i + 1
            emit_load(i + NW)

        pcount = [0]

        def proj_bank():
            b = pcount[0] % 2
            pcount[0] += 1
            return b

        def mm(o, l, r_, start, stop, r, w, signal=True):
            return T.op("pe", lambda: PE.matmul(o, l, r_, start=start, stop=stop), r=r, w=w, signal=signal)

        def proj_fm(name, evac):
            wv, bw = wunit(name)
            bk = proj_bank()
            for k in range(16):
                mm(pb[bk][:], wv[:, k, :], hT[:, k, :], k == 0, k == 15, bw + [BhT], [Bpb[bk]], signal=(k == 15))
            evac(pb[bk], Bpb[bk])

        xcnt = [0]
        thc = [0]

        def tanh_evac(pbk, bpbk):
            s = thc[0] % 2
            thc[0] += 1
            T.op("act", lambda: A.activation(out=tht[:, s, :], in_=pbk[:], func=AF.Tanh, scale=0.5), r=[bpbk],
                 w=[Btht[s]])
            return tht[:, s, :], Btht[s]

        xslot = {}
        deferred = []

        def phase_x_pre(t, b):
            seq_, ti_ = divmod(t, NT)
            r0 = seq_ * S + ti_ * TT + 128 * b
            s = b % 2
            xslot[b] = s
            T.dma("sp", xa[:, s, :], x[r0:r0 + 128, :], dxl[s], w=[Bxa[s]])
            T.op("act", lambda: A.activation(out=hb[:, s, :], in_=xa[:, s, :], func=AF.Square,
                                             accum_out=ss[:, b:b + 1]), r=[Bxa[s]], w=[BarC[s], Bss[b]])
            T.op("pool", lambda: G.tensor_scalar(out=rs[:, b:b + 1], in0=ss[:, b:b + 1], scalar1=1.0 / D,
                                                 scalar2=EPS, op0=ALU.mult, op1=ALU.add), r=[Bss[b]], w=[Brs[b]])
            T.op("pool", lambda: G.tensor_tensor(out=rs[:, b:b + 1], in0=rs[:, b:b + 1], in1=mhalf[:],
                                                 op=ALU.pow), r=[Brs[b], Bc], w=[Brs[b]])
            T.op("dve", lambda: V.tensor_scalar(out=hb[:, s, :], in0=xa[:, s, :], scalar1=rs[:, b:b + 1],
                                                scalar2=None, op0=ALU.mult), r=[Bxa[s], Brs[b]], w=[BarC[s]])

        def phase_x_tr(b):
            s = xslot[b]
            for q4 in range(4):
                tb = 7 - (q4 % 2)
                for i in range(4):
                    kc = q4 * 4 + i
                    T.op("pe", lambda: PE.transpose(ptbs[tb][:, i, :], hb[:, s, kc * 128:(kc + 1) * 128],
                                                    ident[:]), r=[BarC[s], Bc], w=[Bpb[tb]], signal=(i == 3))
                T.op("dve", lambda: V.tensor_tensor(
                    out=hT[:, q4 * 4:(q4 + 1) * 4, b * 128:(b + 1) * 128], in0=ptbs[tb][:, 0:4, :],
                    in1=npre[:, q4 * 4:(q4 + 1) * 4].unsqueeze(2).broadcast_to([128, 4, 128]), op=ALU.mult),
                     r=[Bpb[tb], Bc], w=[BhT])

        def do_tile(t):
            if STOP == "prepass":
                return
            seq, ti = divmod(t, NT)
            tok0 = seq * S + ti * TT
            first = ti == 0

            if t == 0:
                for b in range(4):
                    phase_x_pre(t, b)
                    phase_x_tr(b)
                for i in range(NW):
                    emit_load(i)
            if STOP == "X":
                return
            wv, bw = wmov(["AV0", "AV1"])
            for b in range(4):
                bk = proj_bank()
                for k in range(16):
                    mm(pb[bk][:, 0:256].rearrange("p (u c) -> p u c", c=128), hT[:, k, b * 128:(b + 1) * 128],
                       wv[:, :, k, :], k == 0, k == 15, bw + [BhT], [Bpb[bk]], signal=(k == 15))
                T.op("act", lambda: A.copy(Vaug[:, 1 + b, :, 0:64],
                                           pb[bk][:, 0:256].rearrange("p (g c) -> p g c", c=64)), r=[Bpb[bk]], w=[BVa])

            def kd_evac(g):
                return lambda pbk, bpbk: T.op("act", lambda: A.copy(KT[:, g, 128:640], pbk[:]), r=[bpbk], w=[BKT])

            proj_fm("KD0", kd_evac(0))
            proj_fm("KD1", kd_evac(1))
            wdone()
            for half in range(2):
                names = [f"HI{4 * half + k}" for k in range(4)]
                wv, bw = wmov(names)
                for b in range(4):
                    bk = proj_bank()
                    for k in range(16):
                        mm(pb[bk][:].rearrange("p (u c) -> p u c", c=128), hT[:, k, b * 128:(b + 1) * 128],
                           wv[:, :, k, :], k == 0, k == 15, bw + [BhT], [Bpb[bk]], signal=(k == 15))
                    T.op("dve", lambda: V.tensor_copy(Vh[:, b, half * 512:(half + 1) * 512], pb[bk][:]),
                         r=[Bpb[bk]], w=[BVh])
                wdone()
            while deferred:
                deferred.pop(0)()
            gcount = [0]

            def unit_done():
                gcount[0] += 1
                if gcount[0] % 4 == 0:
                    wdone()

            enter(Batt)
            proj_fm("KD2", kd_evac(2))
            proj_fm("KD3", kd_evac(3))
            wdone()

            def q_unit(g, j):
                qs = g % 2
                proj_fm(f"Q{g}{j}", lambda pbk, bpbk: T.op(
                    "dve", lambda: V.tensor_copy(QT[:, qs, j, :], pbk[:]), r=[bpbk], w=[BQT[qs]]))
                unit_done()

            def ag_unit(g, j):
                qs = g % 2

                def ag_evac(pbk, bpbk):
                    th, bth = tanh_evac(pbk, bpbk)
                    T.op("dve", lambda: V.scalar_tensor_tensor(out=AGs[:, qs, j, :], in0=th, scalar=1.0, in1=pbk[:],
                                                               op0=ALU.add, op1=ALU.mult), r=[bth, bpbk],
                         w=[BAGs[qs]])
                proj_fm(f"AG{g}{j}", ag_evac)
                unit_done()

            def stage_a(g, n):
                qs = g % 2
                blks = [1] if (first and n == 0) else [0, 1]
                b0 = blks[0]
                qc = slice(n * 128, (n + 1) * 128)
                ps_ = n % 2
                sbk = (2, 3) if n % 2 == 0 else (5, 6)
                for p in range(2):
                    pr = slice(p * 64, (p + 1) * 64)
                    bk = sbk[p]
                    for bi, blk in enumerate(blks):
                        mm(pb[bk][:, blk * 256:(blk + 1) * 256].rearrange("p (j q) -> p j q", j=2),
                           KT[pr, g, (n + blk) * 128:(n + blk + 1) * 128], QT[pr, qs, :, qc], True, True,
                           [BKT, BQT[qs]], [Bpb[bk]], signal=(bi == len(blks) - 1))
                    T.op("act", lambda: A.activation(out=Pe[:, ps_, p * 512 + b0 * 256:(p + 1) * 512],
                                                     in_=pb[bk][:, b0 * 256:512], func=AF.Exp, scale=0.125),
                         r=[Bpb[bk]], w=[BPe[ps_]])
                for p in range(2):
                    cs_ = slice(p * 512 + b0 * 256, (p + 1) * 512)
                    if p == 0:
                        T.op("dve", lambda: V.tensor_tensor(out=PT[:, ps_, cs_], in0=Pe[:, ps_, cs_],
                                                            in1=Emb[:, g, cs_], op=ALU.mult),
                             r=[BPe[ps_], Bc], w=[BPT[ps_]])
                    else:
                        T.op("pool", lambda: G.tensor_tensor(out=PT[:, ps_, cs_], in0=Pe[:, ps_, cs_],
                                                             in1=Emb[:, g, cs_], op=ALU.mult),
                             r=[BPe[ps_], Bc], w=[BPT2[ps_]])

            def stage_b(g, n):
                blks = [1] if (first and n == 0) else [0, 1]
                ps_ = n % 2
                po = pb[4][:].rearrange("p (h c) -> p h c", c=128)
                ptv = PT[:, ps_, :].rearrange("p (a b j q) -> p a b j q", a=2, b=2, j=2)
                for j in range(2):
                    for p in range(2):
                        hs = 2 * j + p
                        for bi, blk in enumerate(blks):
                            last = (j == 1 and p == 1 and bi == len(blks) - 1)
                            mm(po[:, hs, 0:65], ptv[:, p, blk, j, :], Vaug[:, n + blk, g, 0:65], bi == 0,
                               bi == len(blks) - 1, [BPT[ps_], BPT2[ps_], BVa], [Bpb[4]], signal=last)
                T.op("dve", lambda: V.tensor_tensor(out=dd[:, ps_, :], in0=po[:, :, 64],
                                                    in1=es2[:, 4 * g:4 * g + 4], op=ALU.add),
                     r=[Bpb[4], Bc], w=[Bdd[ps_]])
                T.op("dve", lambda: V.reciprocal(rd[:, ps_, :], dd[:, ps_, :]), r=[Bdd[ps_]], w=[Bdd[ps_]])
                T.op("dve", lambda: V.tensor_tensor(
                    out=yatm[:, ps_, :].rearrange("p (h c) -> p h c", c=64), in0=po[:, :, 0:64],
                    in1=rd[:, ps_, :].unsqueeze(2).broadcast_to([128, 4, 64]), op=ALU.mult),
                     r=[Bpb[4], Bdd[ps_]], w=[Byatm[ps_]])

            def stage_c(g, n):
                qs = g % 2
                ps_ = n % 2
                qc = slice(n * 128, (n + 1) * 128)
                for j in range(2):
                    T.op("pe", lambda: PE.transpose(ptb[:, j, :], yatm[:, ps_, j * 128:(j + 1) * 128], ident[:]),
                         r=[Byatm[ps_], Bc], w=[Bpb[7]], signal=(j == 1))
                T.op("dve", lambda: V.tensor_tensor(out=yaT[:, 2 * g:2 * g + 2, qc], in0=ptb[:, 0:2, :],
                                                    in1=AGs[:, qs, :, qc], op=ALU.mult),
                     r=[Bpb[7], BAGs[qs]], w=[ByaT])

            lnf = arE32[:, 0:512]
            Gc = arE32[:, 512:1024]
            eG = arE32[:, 1024:1536]
            erG = arE32[:, 1536:2048]
            k1 = arE32[:, 2048:2560]
            kk = arE32[:, 2560:3072]
            Blnf, BGc, BeG, BerG, Bk1, Bkk = BH2

            def h2a(h):
                T.op("act", lambda: A.activation(out=lnf, in_=thf[:, h, :], func=AF.Ln, scale=coef[:, 1, h:h + 1],
                                                 bias=coef[:, 0, h:h + 1]), r=[BarB, Bc], w=[Blnf])
                T.op("pool", lambda: G.tensor_scalar(out=k1, in0=thf[:, h, :], scalar1=coef[:, 2, h:h + 1],
                                                     scalar2=coef[:, 1, h:h + 1], op0=ALU.mult, op1=ALU.add),
                     r=[BarB, Bc], w=[Bk1])
                for c in range(8):
                    cs = slice(c * 64, (c + 1) * 64)
                    T.op("dve", lambda: V.tensor_tensor_scan(out=Gc[:, cs], data0=lnf[:, cs], data1=zeros[:],
                                                             initial=0.0, op0=ALU.add, op1=ALU.add),
                         r=[Blnf, Bc], w=[BGc])

            def h2b(h):
                T.op("act", lambda: A.activation(out=eG, in_=Gc, func=AF.Exp), r=[BGc], w=[BeG])
                T.op("act", lambda: A.activation(out=erG, in_=Gc, func=AF.Exp, scale=-1.0), r=[BGc], w=[BerG])
                T.op("dve", lambda: V.scalar_tensor_tensor(out=QaT[:, h, :], in0=Qs2[:, h, :], scalar=0.5, in1=eG,
                                                           op0=ALU.mult, op1=ALU.mult), r=[ByhT, BeG], w=[BQaT])
                T.op("dve", lambda: V.tensor_copy(Plast[:, h, :], eG.rearrange("p (c t) -> p c t", t=64)[:, :, 63]),
                     r=[BeG], w=[BPl])
                T.op("pool", lambda: G.tensor_tensor(out=kk, in0=k1, in1=erG, op=ALU.mult), r=[Bk1, BerG], w=[Bkk])
                T.op("pool", lambda: G.tensor_copy(KaT[:, h, :], kk), r=[Bkk], w=[BarD])
                T.op("dve", lambda: V.tensor_tensor(
                    out=KdT[:, h, :].rearrange("p (c t) -> p c t", t=64), in0=kk.rearrange("p (c t) -> p c t", t=64),
                    in1=Plast[:, h, :].unsqueeze(2).broadcast_to([128, 8, 64]), op=ALU.mult),
                     r=[Bkk, BPl], w=[BarD])

            def hq_unit(h):
                def hq_evac(pbk, bpbk):
                    th, bth = tanh_evac(pbk, bpbk)
                    T.op("dve", lambda: V.scalar_tensor_tensor(out=Qs2[:, h, :], in0=th, scalar=1.0, in1=pbk[:],
                                                               op0=ALU.add, op1=ALU.mult), r=[bth, bpbk], w=[ByhT])
                proj_fm(f"HQ{h}", hq_evac)
                unit_done()

            def hf_unit(h):
                proj_fm(f"HF{h}", lambda pbk, bpbk: T.op(
                    "act", lambda: A.activation(out=thf[:, h, :], in_=pbk[:], func=AF.Tanh, scale=0.5), r=[bpbk],
                    w=[BarB]))
                unit_done()

            def hg_unit(h):
                def hg_evac(pbk, bpbk):
                    th, bth = tanh_evac(pbk, bpbk)
                    T.op("dve", lambda: V.scalar_tensor_tensor(out=Gs2[:, h, :], in0=th, scalar=1.0, in1=pbk[:],
                                                               op0=ALU.add, op1=ALU.mult), r=[bth, bpbk], w=[BGs2])
                proj_fm(f"HG{h}", hg_evac)
                unit_done()

            hstream = []
            for h in range(8):
                hstream.append(lambda h=h: hq_unit(h))
                if h == 0:
                    hstream.append(lambda h=h: hf_unit(h))
                else:
                    hstream.append(lambda h=h: (hf_unit(h), h2b_guard(h - 1)))
                hstream.append(lambda h=h: (hg_unit(h), h2a_guard(h)))
            hpos = [0]
            h2state = {"a": False, "b": False, "att_done": False}

            def h2a_guard(h):
                if not h2state["a"]:
                    enter(BH2)
                    h2state["a"] = True
                h2a(h)

            def h2b_guard(h):
                assert h2state["att_done"]
                if not h2state["b"]:
                    enter([BarD])
                    h2state["b"] = True
                h2b(h)

            def next_h():
                f_ = hstream[hpos[0]]
                hpos[0] += 1
                f_()

            q_unit(0, 0)
            q_unit(0, 1)
            ag_unit(0, 0)
            ag_unit(0, 1)
            for g in range(4):
                if g < 3:
                    fill = [lambda g=g: q_unit(g + 1, 0), lambda g=g: q_unit(g + 1, 1),
                            lambda g=g: ag_unit(g + 1, 0), lambda g=g: ag_unit(g + 1, 1)]
                else:
                    fill = [next_h, next_h, next_h, next_h]
                if g == 0:
                    stage_a(g, 0)
                for n in range(4):
                    if n + 1 < 4:
                        stage_a(g, n + 1)
                    elif g + 1 < 4:
                        stage_a(g + 1, 0)
                    stage_b(g, n)
                    fill[n]()
                    stage_c(g, n)
            T.op("pool", lambda: G.tensor_copy(KT[:, :, 0:128], KT[:, :, 512:640]), r=[BKT], w=[BKT])
            T.op("pool", lambda: G.tensor_copy(Vaug[:, 0, :, 0:64], Vaug[:, 4, :, 0:64]), r=[BVa], w=[BVa])
            h2state["att_done"] = True
            if STOP == "attn":
                return
            while hpos[0] < len(hstream):
                next_h()
            h2b_guard(7)
            if gcount[0] % 4 != 0:
                wdone()

            if STOP == "h2":
                return
            if first:
                T.op("pool", lambda: G.memset(S32[:], 0.0), w=BS32)
                T.op("pool", lambda: G.memset(Sbf[:], 0.0), w=[bb for pr_ in BSb for bb in pr_])
            enter(Bsq + Blr + Bot)
            tha = tha_t[:]
            ecnt = [0]

            def early():
                c = ecnt[0]
                if c >= 16:
                    return
                ecnt[0] += 1
                s2 = c % 2
                wga, bwga = wunit(f"GA{c}")
                wwa, bwwa = wunit(f"WA{c}")
                for k in range(16):
                    mm(pb[4][:], wga[:, k, :], hT[:, k, :], k == 0, k == 15, bwga + [BhT], [Bpb[4]], signal=(k == 15))
                T.op("act", lambda: A.activation(out=tha[:, s2, :], in_=pb[4][:], func=AF.Tanh, scale=0.5),
                     r=[Bpb[4]], w=[Btha[s2]])
                for k in range(8):
                    mm(pb[5][:], wwa[:, k, :], yaT[:, k, :], k == 0, k == 7, bwwa + [ByaT], [Bpb[5]], signal=(k == 7))
                T.op("dve", lambda: V.scalar_tensor_tensor(out=mergedT[:, c, :], in0=tha[:, s2, :], scalar=1.0,
                                                           in1=pb[5][:], op0=ALU.add, op1=ALU.mult),
                     r=[Btha[s2], Bpb[5]], w=[BarB])
                if c % 2 == 1:
                    wdone()

            early()
            early()
            sq = arE[:, 0:1024].rearrange("p (s c) -> p s c", c=128)
            lr = arE32[:, 512:1536].rearrange("p (g c) -> p g c", c=512)
            ot = arE32[:, 1536:2560].rearrange("p (g c) -> p g c", c=512)
            for b in range(4):
                bc = slice(b * 128, (b + 1) * 128)
                for hg in range(2):
                    tb, ab = 7 - hg, 2 + hg
                    for hh in range(4):
                        h = 4 * hg + hh
                        T.op("pe", lambda: PE.transpose(ptbs[tb][:, hh, :], KdT[:, h, bc], ident[:]), r=[BarD, Bc],
                             w=[Bpb[tb]], signal=(hh == 3))
                    for hh in range(4):
                        h = 4 * hg + hh
                        mm(pb[ab][:, hh * 128:(hh + 1) * 128], KaT[:, h, bc], QaT[:, h, bc], True, True,
                           [BarD, BQaT], [Bpb[ab]], signal=(hh == 3))
                    T.op("act", lambda: A.copy(Kdtm[:, 4 * hg:4 * hg + 4, :], ptbs[tb][:, 0:4, :]), r=[Bpb[tb]],
                         w=BKdtm[4 * hg:4 * hg + 4])
                    T.op("dve", lambda: V.tensor_tensor(
                        out=Am[:, 4 * hg:4 * hg + 4, :], in0=pb[ab][:].rearrange("p (h c) -> p h c", c=128),
                        in1=hmask[:].unsqueeze(1).broadcast_to([128, 4, 128]), op=ALU.mult),
                         r=[Bpb[ab], Bc], w=BAm[4 * hg:4 * hg + 4])
                early()
                for hg in range(2):
                    u0, u1 = 0, 6 + hg
                    for hh in range(4):
                        h = 4 * hg + hh
                        hc = slice(h * 128, (h + 1) * 128)
                        mm(pb[u0][:, hh * 128:(hh + 1) * 128], Kdtm[0:64, h, :], Vh[0:64, b, hc], True, True,
                           [BKdtm[h], BVh], [Bpb[u0]], signal=False)
                        mm(pb[u1][:, hh * 128:(hh + 1) * 128], Kdtm[64:128, h, :], Vh[64:128, b, hc], True, True,
                           [BKdtm[h], BVh], [Bpb[u1]], signal=(hh == 3))
                    for hh in range(4):
                        h = 4 * hg + hh
                        T.op("dve", lambda: V.scalar_tensor_tensor(
                            out=S32[:, h, :], in0=S32[:, h, :], scalar=Plast[:, h, 2 * b:2 * b + 1],
                            in1=pb[u0][:, hh * 128:(hh + 1) * 128], op0=ALU.mult, op1=ALU.add),
                             r=[BS32[h], BPl, Bpb[u0]], w=[BS32[h]])
                        T.op("act", lambda: A.copy(Sbf[:, h, 0, :], S32[:, h, :]), r=[BS32[h]], w=[BSb[h][0]])
                early()
                for hg in range(2):
                    u1, ob = 6 + hg, 2 + hg
                    for hh in range(4):
                        h = 4 * hg + hh
                        hc = slice(h * 128, (h + 1) * 128)
                        oc_ = slice(hh * 128, (hh + 1) * 128)
                        mm(pb[ob][:, oc_], Vh[:, b, hc], Am[:, h, :], True, False, [BVh, BAm[h]], [Bpb[ob]],
                           signal=False)
                        mm(pb[ob][:, hh * 128:hh * 128 + 64], Sbf[:, h, 1, :], QaT[:, h, b * 128:b * 128 + 64],
                           False, False, [BSb[h][1], BQaT], [Bpb[ob]], signal=False)
                        mm(pb[ob][:, hh * 128 + 64:(hh + 1) * 128], Sbf[:, h, 0, :],
                           QaT[:, h, b * 128 + 64:b * 128 + 128], False, True, [BSb[h][0], BQaT], [Bpb[ob]])
                    for hh in range(4):
                        h = 4 * hg + hh
                        T.op("dve", lambda: V.scalar_tensor_tensor(
                            out=S32[:, h, :], in0=S32[:, h, :], scalar=Plast[:, h, 2 * b + 1:2 * b + 2],
                            in1=pb[u1][:, hh * 128:(hh + 1) * 128], op0=ALU.mult, op1=ALU.add),
                             r=[BS32[h], BPl, Bpb[u1]], w=[BS32[h]])
                        T.op("act", lambda: A.copy(Sbf[:, h, 1, :], S32[:, h, :]), r=[BS32[h]], w=[BSb[h][1]])
                    T.op("act", lambda: A.activation(out=sq[:, 4 * hg:4 * hg + 4, :],
                                                     in_=pb[ob][:].rearrange("p (h c) -> p h c", c=128),
                                                     func=AF.Square), r=[Bpb[ob]], w=[Bsq[hg]])
                early()
                for hg in range(2):
                    ob, sb_ = 2 + hg, 6 + hg
                    mm(pb[sb_][:], ones[:], sq[:, 4 * hg:4 * hg + 4, :], True, True, [Bc, Bsq[hg]], [Bpb[sb_]])
                    T.op("act", lambda: A.activation(out=lr[:, hg, :], in_=pb[sb_][:], func=AF.Ln,
                                                     scale=1.0 / 128.0, bias=epsb[:]), r=[Bpb[sb_], Bc], w=[Blr[hg]])
                    T.op("act", lambda: A.activation(out=lr[:, hg, :], in_=lr[:, hg, :], func=AF.Exp, scale=-0.5),
                         r=[Blr[hg]], w=[Blr[hg]])
                    for hh in range(4):
                        h = 4 * hg + hh
                        oc_ = slice(hh * 128, (hh + 1) * 128)
                        T.op("dve", lambda: V.scalar_tensor_tensor(out=ot[:, hg, oc_], in0=pb[ob][:, oc_],
                                                                   scalar=coef[:, 3, h:h + 1], in1=lr[:, hg, oc_],
                                                                   op0=ALU.mult, op1=ALU.mult),
                             r=[Bpb[ob], Bc, Blr[hg]], w=[Bot[hg]])
                    T.op("pool", lambda: G.tensor_tensor(
                        out=yhT[:, 4 * hg:4 * hg + 4, bc], in0=ot[:, hg, :].rearrange("p (h c) -> p h c", c=128),
                        in1=Gs2[:, 4 * hg:4 * hg + 4, bc], op=ALU.mult), r=[Bot[hg], BGs2], w=[ByhT])
                early()

            if STOP == "h3":
                return
            while ecnt[0] < 16:
                early()
            enter([BarE])
            thh = arE[:, 0:1024].rearrange("p (s c) -> p s c", c=TT)
            m2 = arE32[:, 1024:2048].rearrange("p (s c) -> p s c", c=TT)
            for c in range(16):
                s2 = c % 2
                wgh, bwgh = wunit(f"GH{c}")
                wwh, bwwh = wunit(f"WH{c}")
                bgh, buh = 0 + s2, 2 + s2
                for k in range(16):
                    mm(pb[bgh][:], wgh[:, k, :], hT[:, k, :], k == 0, k == 15, bwgh + [BhT], [Bpb[bgh]], signal=(k == 15))
                T.op("act", lambda: A.activation(out=thh[:, s2, :], in_=pb[bgh][:], func=AF.Tanh, scale=0.5),
                     r=[Bpb[bgh]], w=[BarE])
                for k in range(8):
                    mm(pb[buh][:], wwh[:, k, :], yhT[:, k, :], k == 0, k == 7, bwwh + [ByhT], [Bpb[buh]], signal=(k == 7))
                T.op("dve", lambda: V.scalar_tensor_tensor(out=m2[:, s2, :], in0=thh[:, s2, :], scalar=1.0,
                                                           in1=pb[buh][:], op0=ALU.add, op1=ALU.mult),
                     r=[BarE, Bpb[buh]], w=[BarE])
                T.op("pool", lambda: G.tensor_tensor(out=mergedT[:, c, :], in0=mergedT[:, c, :], in1=m2[:, s2, :],
                                                     op=ALU.add), r=[BarE], w=[BarB])
                if c % 2 == 1:
                    wdone()

            if STOP == "merge":
                return
            enter([Bysb])
            nxt = t + 1 if t + 1 < NSEQ * NT else None
            if nxt is not None:
                phase_x_pre(nxt, 0)
            xres = [xa[:, 0, :], xa[:, 1, :], arC[:].bitcast(F32), xa[:, 0, :]]
            Bres = [[Bxa[0]], [Bxa[1]], BarC, [Bxa[0]]]
            dres = [dxl[0], dxl[1], dxl2, dxl[0]]
            dsto = [dxs[0], dxs[1], dxs2, dxs[0]]

            def reload(b):
                r0_ = tok0 + 128 * b
                T.dma("sp", xres[b], x[r0_:r0_ + 128, :], dres[b], w=Bres[b])

            oc = [0]
            for i in range(4):
                wv, bw = wmov([f"WO{4 * i + k}" for k in range(4)])
                for b in range(4):
                    bk = oc[0] % 6
                    oc[0] += 1
                    for k in range(16):
                        mm(pb[bk][:].rearrange("p (u c) -> p u c", c=128), mergedT[:, k, b * 128:(b + 1) * 128],
                           wv[:, :, k, :], k == 0, k == 15, bw + [BarB], [Bpb[bk]], signal=(k == 15))
                    T.op("act", lambda: A.activation(out=junk[:], in_=pb[bk][:], func=AF.Square,
                                                     accum_out=ssq[:, b, i:i + 1]), r=[Bpb[bk]], w=[Bjunk, Bssq[b]])
                    T.op("dve", lambda: V.tensor_tensor(out=ysb[:, b, i * 512:(i + 1) * 512], in0=pb[bk][:],
                                                        in1=npost[:, i * 512:(i + 1) * 512], op=ALU.mult),
                         r=[Bpb[bk], Bc], w=[Bysb])
                if nxt is not None:
                    phase_x_tr(i)
                wdone()
                if nxt is not None and i + 1 < 4:
                    phase_x_pre(nxt, i + 1)
                if i == 1:
                    reload(0)
                if i == 2:
                    reload(1)
            reload(2)

            def tail(b):
                r0 = tok0 + 128 * b
                T.op("dve", lambda: V.reduce_sum(out=ss2[:, b:b + 1], in_=ssq[:, b, :], axis=mybir.AxisListType.X),
                     r=[Bssq[b]], w=[Bss2[b]])
                T.op("pool", lambda: G.tensor_scalar(out=rs2[:, b:b + 1], in0=ss2[:, b:b + 1], scalar1=1.0 / D,
                                                     scalar2=4.0 * EPS, op0=ALU.mult, op1=ALU.add),
                     r=[Bss2[b]], w=[Brs2[b]])
                T.op("pool", lambda: G.tensor_tensor(out=rs2[:, b:b + 1], in0=rs2[:, b:b + 1], in1=mhalf[:],
                                                     op=ALU.pow), r=[Brs2[b], Bc], w=[Brs2[b]])
                T.op("dve", lambda: V.scalar_tensor_tensor(out=ysb[:, b, :], in0=ysb[:, b, :], scalar=rs2[:, b:b + 1],
                                                           in1=xres[b], op0=ALU.mult, op1=ALU.add),
                     r=[Brs2[b]] + Bres[b], w=[Bysb])
                T.dma("sp", out[r0:r0 + 128, :], ysb[:, b, :], dsto[b], r=[Bysb])

            for b in range(3):
                tail(b)
            reload(3)
            deferred.append(lambda: tail(3))

        epsb = sb("epsb", [128, 1], F32)
        T.op("pool", lambda: G.memset(epsb[:], EPS), w=[Bc])

        for t in range(NSEQ * NT):
            do_tile(t)
        while deferred:
            deferred.pop(0)()
        T.barrier(("sp",))
    return nc


def _t5_bucket(dist):
    max_exact = 16
    d = np.maximum(dist, 0)
    df = np.maximum(d, 1).astype(np.float32)
    large = max_exact + (np.log(df / max_exact) / np.log(128 / max_exact) * (32 - max_exact)).astype(np.int32)
    large = np.minimum(large, 31)
    return np.where(d < max_exact, d, large)


def host_consts(norm_pre, norm_post, rel_bias, attn_sinks, lb_logits, hgrn_norm):
    f32 = np.float32
    s = np.arange(128)[:, None]
    q = np.arange(128)[None, :]
    biasg = np.zeros((128, 4, 2, 2, 2, 128), f32)
    amask = np.zeros((128, 4, 2, 2, 2, 128), f32)
    for blk in range(2):
        dist = q + 128 - (s + 128 * blk)
        valid = (dist >= 0) & (dist < 128)
        bk = _t5_bucket(dist)
        for g in range(4):
            for p in range(2):
                for j in range(2):
                    head = 4 * g + 2 * j + p
                    biasg[:, g, p, blk, j, :] = rel_bias[bk, head]
                    amask[:, g, p, blk, j, :] = valid
    hm = ((q >= s) & ((q // 64) == (s // 64))).astype(f32)
    return {
        "npre": np.ascontiguousarray(norm_pre.reshape(16, 128).T).astype(f32),
        "npost": np.ascontiguousarray(np.broadcast_to(norm_post.reshape(1, D), (128, D))).astype(f32),
        "biasg": biasg.reshape(128, 4096),
        "amask": amask.reshape(128, 4096),
        "sinks": np.ascontiguousarray(np.broadcast_to(attn_sinks.reshape(1, 16), (128, 16))).astype(f32),
        "lbl": np.ascontiguousarray(lb_logits.reshape(2, 8, 128).transpose(2, 0, 1).reshape(128, 16)).astype(f32),
        "hgain": np.ascontiguousarray(hgrn_norm.reshape(8, 128).T).astype(f32),
        "ident": np.eye(128, dtype=f32).astype(ml_dtypes.bfloat16),
        "hmask": hm.astype(ml_dtypes.bfloat16),
    }


_NC_CACHE = {}


def run(x, norm_pre, w_in, rel_bias, attn_sinks, lb_logits, hgrn_norm, w_branch_attn, w_branch_hgrn, w_out,
        norm_post, n_cores):
    x = np.asarray(x, np.float32)
    B, S, _ = x.shape
    assert B % n_cores == 0
    nseq = B // n_cores
    key = (nseq, S)
    if key not in _NC_CACHE:
        _NC_CACHE[key] = build_nc(nseq, S)
    nc = _NC_CACHE[key]
    consts = host_consts(np.asarray(norm_pre, np.float32)[0], np.asarray(norm_post, np.float32)[0],
                         np.asarray(rel_bias, np.float32), np.asarray(attn_sinks, np.float32)[0],
                         np.asarray(lb_logits, np.float32), np.asarray(hgrn_norm, np.float32)[0])
    shared = dict(consts)
    shared["w_in"] = np.ascontiguousarray(np.asarray(w_in, np.float32)[0])
    shared["w_a"] = np.ascontiguousarray(np.asarray(w_branch_attn, np.float32)[0])
    shared["w_h"] = np.ascontiguousarray(np.asarray(w_branch_hgrn, np.float32)[0])
    shared["w_o"] = np.ascontiguousarray(np.asarray(w_out, np.float32)[0])
    in_maps = []
    for c in range(n_cores):
        m = dict(shared)
        m["x"] = np.ascontiguousarray(x[c * nseq:(c + 1) * nseq].reshape(nseq * S, D))
        in_maps.append(m)
    res = run_bass_kernel_spmd(nc, in_maps, core_ids=list(range(n_cores)))
    outs = [np.asarray(r["out"], np.float32).reshape(nseq, S, D) for r in res.results]
    return np.concatenate(outs, axis=0)


def kernel(x, norm_pre, w_in, rel_bias, attn_sinks, lb_logits, hgrn_norm, w_branch_attn, w_branch_hgrn, w_out,
           norm_post):
    return run(x, norm_pre, w_in, rel_bias, attn_sinks, lb_logits, hgrn_norm, w_branch_attn, w_branch_hgrn, w_out,
               norm_post, n_cores=8)
```
